# Optimizing a Trainium2 kernel written in Bass

```python
import math
import jax
import jax.numpy as jnp
from jax import lax
import numpy as np

D_MODEL = 4096
BATCH = 1
SEQ = 16384
DEPTH = 4

CTX_LEN = 256
GRID_W = 64
N_MIXERS = 4
ROPE_BASE = 10000.0
LN_EPS = 1e-6
NEG_INF = -1e30
DN_ALPHA = (2 * DEPTH) ** 0.25
DN_BETA = (8 * DEPTH) ** -0.25

WA_HEADS = 32
WA_KV_HEADS = 8
WA_HEAD_DIM = D_MODEL // WA_HEADS
WA_WINDOW = 128
WA_BLOCK = 128
WA_Q_W = WA_HEADS * WA_HEAD_DIM
WA_KV_W = WA_KV_HEADS * WA_HEAD_DIM
WA_IN_W = 2 * WA_KV_W + 2 * WA_Q_W

RT_HEADS = 16
RT_KEY_DIM = D_MODEL // RT_HEADS
RT_VAL_DIM = 2 * RT_KEY_DIM
RT_QK_W = RT_HEADS * RT_KEY_DIM
RT_V_W = RT_HEADS * RT_VAL_DIM
RT_IN_W = 2 * RT_QK_W + 2 * RT_V_W
RT_CHUNK = 128

NA_HEADS = 32
NA_HEAD_DIM = D_MODEL // NA_HEADS
NA_W = NA_HEADS * NA_HEAD_DIM
NA_IN_W = 4 * NA_W
NA_KH = 8
NA_KW = 16

S5_WIDTH = D_MODEL
S5_GROUP = 16
S5_GROUPS = S5_WIDTH // S5_GROUP
S5_STATE = 64
S5_CHUNK = 128
S5_IN_W = 2 * S5_WIDTH

kernel_name = 'hybrid_prefix_diffusion_backbone'


def _layer_norm(x, g, b):
    xf = x.astype(jnp.float32)
    mu = xf.mean(-1, keepdims=True)
    var = jnp.square(xf - mu).mean(-1, keepdims=True)
    return ((xf - mu) * lax.rsqrt(var + LN_EPS) * g.astype(jnp.float32) + b.astype(jnp.float32)).astype(x.dtype)


def _rope_1d(x, pos):
    half = x.shape[-1] // 2
    inv_freq = ROPE_BASE ** (-jnp.arange(half, dtype=jnp.float32) / half)
    ang = pos.astype(jnp.float32)[:, None] * inv_freq[None, :]
    cos = jnp.cos(ang)[None, :, None, :]
    sin = jnp.sin(ang)[None, :, None, :]
    x1 = x[..., :half].astype(jnp.float32)
    x2 = x[..., half:].astype(jnp.float32)
    return jnp.concatenate([x1 * cos - x2 * sin, x2 * cos + x1 * sin], axis=-1).astype(x.dtype)


def _rope_2d(x, rows, cols):
    half = x.shape[-1] // 2
    return jnp.concatenate([_rope_1d(x[..., :half], rows), _rope_1d(x[..., half:], cols)], axis=-1)


def _joint_softmax(scores, sink=None):
    parts = list(scores)
    if sink is not None:
        parts.append(jnp.broadcast_to(sink, parts[0].shape[:-1] + (1,)))
    p = jax.nn.softmax(jnp.concatenate(parts, axis=-1), axis=-1)
    out, start = [], 0
    for s in scores:
        n = s.shape[-1]
        out.append(p[..., start:start + n])
        start += n
    return out


def _mixer_window_gqa(h_lat, h_ctx, w_in, w_out, sink, ctx_out):
    B, L, _ = h_lat.shape
    n_ctx = h_ctx.shape[1]
    G = WA_HEADS // WA_KV_HEADS
    scale = WA_HEAD_DIM ** -0.5
    pos = jnp.arange(L)
    rows, cols = pos // GRID_W, pos % GRID_W
    p = h_lat @ w_in
    k = _rope_2d(p[..., :WA_KV_W].reshape(B, L, WA_KV_HEADS, WA_HEAD_DIM), rows, cols)
    v = p[..., WA_KV_W:2 * WA_KV_W].reshape(B, L, WA_KV_HEADS, WA_HEAD_DIM)
    q = _rope_2d(p[..., 2 * WA_KV_W:2 * WA_KV_W + WA_Q_W].reshape(B, L, WA_HEADS, WA_HEAD_DIM), rows, cols)
    q = (q * scale).reshape(B, L, WA_KV_HEADS, G, WA_HEAD_DIM)
    z = p[..., 2 * WA_KV_W + WA_Q_W:]
    pc = h_ctx @ (w_in if ctx_out else w_in[:, :2 * WA_KV_W])
    kc = pc[..., :WA_KV_W].reshape(B, n_ctx, WA_KV_HEADS, WA_HEAD_DIM)
    vc = pc[..., WA_KV_W:2 * WA_KV_W].reshape(B, n_ctx, WA_KV_HEADS, WA_HEAD_DIM)
    sink_g = sink.astype(jnp.float32).reshape(WA_KV_HEADS, G)[None, :, :, None, None]
    pad = ((0, 0), (WA_BLOCK, WA_BLOCK), (0, 0), (0, 0))
    k_pad, v_pad = jnp.pad(k, pad), jnp.pad(v, pad)
    offs_q = jnp.arange(WA_BLOCK)
    offs_k = jnp.arange(3 * WA_BLOCK) - WA_BLOCK

    def block(j):
        start = j * WA_BLOCK
        qb = lax.dynamic_slice_in_dim(q, start, WA_BLOCK, axis=1)
        kb = lax.dynamic_slice_in_dim(k_pad, start, 3 * WA_BLOCK, axis=1)
        vb = lax.dynamic_slice_in_dim(v_pad, start, 3 * WA_BLOCK, axis=1)
        kpos = start + offs_k
        valid = (jnp.abs(offs_q[:, None] - offs_k[None, :]) <= WA_WINDOW) & ((kpos >= 0) & (kpos < L))[None, :]
        s_loc = jnp.where(valid, jnp.einsum('bqhgd,bkhd->bhgqk', qb, kb).astype(jnp.float32), NEG_INF)
        s_ctx = jnp.einsum('bqhgd,bkhd->bhgqk', qb, kc).astype(jnp.float32)
        p_loc, p_ctx = _joint_softmax([s_loc, s_ctx], sink_g)
        return (jnp.einsum('bhgqk,bkhd->bqhgd', p_loc.astype(vb.dtype), vb)
                + jnp.einsum('bhgqk,bkhd->bqhgd', p_ctx.astype(vc.dtype), vc))

    o = lax.map(block, jnp.arange(L // WA_BLOCK))
    o = jnp.moveaxis(o, 0, 1).reshape(B, L, WA_Q_W)
    y_lat = (o * jax.nn.silu(z)) @ w_out
    if not ctx_out:
        return y_lat, None
    qc = (pc[..., 2 * WA_KV_W:2 * WA_KV_W + WA_Q_W] * scale).reshape(B, n_ctx, WA_KV_HEADS, G, WA_HEAD_DIM)
    zc = pc[..., 2 * WA_KV_W + WA_Q_W:]
    (p_cc,) = _joint_softmax([jnp.einsum('bqhgd,bkhd->bhgqk', qc, kc).astype(jnp.float32)], sink_g)
    oc = jnp.einsum('bhgqk,bkhd->bqhgd', p_cc.astype(vc.dtype), vc).reshape(B, n_ctx, WA_Q_W)
    y_ctx = (oc * jax.nn.silu(zc)) @ w_out
    return y_lat, y_ctx


def _retention_scan(q, k, v, log_g, s0, inclusive):
    B, L, H, _ = k.shape
    n = L // RT_CHUNK
    idx = jnp.arange(RT_CHUNK, dtype=jnp.float32)
    diff = idx[:, None] - idx[None, :]
    mask = (diff >= 0) if inclusive else (diff > 0)
    d_intra = jnp.where(mask[None], jnp.exp(jnp.maximum(diff, 0.0)[None] * log_g[:, None, None]), 0.0)
    q_decay = jnp.exp((idx[:, None] + 1.0) * log_g[None, :])[None, :, :, None]
    k_decay = jnp.exp((RT_CHUNK - 1.0 - idx[:, None]) * log_g[None, :])[None, :, :, None]
    c_decay = jnp.exp(RT_CHUNK * log_g)[None, :, None, None]
    emit = q is not None

    def chunks(t):
        return jnp.moveaxis(t.reshape(B, n, RT_CHUNK, *t.shape[2:]), 1, 0)

    def step(s, xs):
        if emit:
            qc, kc, vc = xs
        else:
            kc, vc = xs
        s_new = c_decay * s + jnp.einsum('bjhd,bjhe->bhde', kc * k_decay, vc)
        if not emit:
            return s_new, None
        att = jnp.einsum('bihd,bjhd->bhij', qc, kc) * d_intra[None]
        o = jnp.einsum('bhij,bjhe->bihe', att, vc) + jnp.einsum('bihd,bhde->bihe', qc * q_decay, s)
        return s_new, o

    xs = (chunks(q), chunks(k), chunks(v)) if emit else (chunks(k), chunks(v))
    s_fin, o = lax.scan(step, s0, xs)
    if emit:
        o = jnp.moveaxis(o, 0, 1).reshape(B, L, H, v.shape[-1])
    return o, s_fin


def _retention_out(o, z, w_out):
    B, L = o.shape[:2]
    mu = o.mean(-1, keepdims=True)
    var = jnp.square(o - mu).mean(-1, keepdims=True)
    y = ((o - mu) * lax.rsqrt(var + LN_EPS)).reshape(B, L, RT_V_W).astype(z.dtype)
    return (y * jax.nn.silu(z)) @ w_out


def _mixer_retention(h_lat, h_ctx, w_in, w_out, log_decay, ctx_out):
    B, L, _ = h_lat.shape
    lg_f = log_decay[0].astype(jnp.float32)
    lg_b = log_decay[1].astype(jnp.float32)
    kscale = RT_KEY_DIM ** -0.5

    def heads(t, d):
        return t.reshape(B, t.shape[1], RT_HEADS, d).astype(jnp.float32)

    s0 = jnp.zeros((B, RT_HEADS, RT_KEY_DIM, RT_VAL_DIM), jnp.float32)
    pc = h_ctx @ (w_in if ctx_out else w_in[:, :RT_QK_W + RT_V_W])
    kc = heads(pc[..., :RT_QK_W], RT_KEY_DIM) * kscale
    vc = heads(pc[..., RT_QK_W:RT_QK_W + RT_V_W], RT_VAL_DIM)
    qc = heads(pc[..., RT_QK_W + RT_V_W:2 * RT_QK_W + RT_V_W], RT_KEY_DIM) if ctx_out else None
    oc_f, sc_f = _retention_scan(qc, kc, vc, lg_f, s0, True)
    oc_b, sc_b = _retention_scan(qc[:, ::-1] if ctx_out else None, kc[:, ::-1], vc[:, ::-1], lg_b, s0, False)
    p = h_lat @ w_in
    pos = jnp.arange(L)
    k = _rope_1d(heads(p[..., :RT_QK_W], RT_KEY_DIM), pos) * kscale
    v = heads(p[..., RT_QK_W:RT_QK_W + RT_V_W], RT_VAL_DIM)
    q = _rope_1d(heads(p[..., RT_QK_W + RT_V_W:2 * RT_QK_W + RT_V_W], RT_KEY_DIM), pos)
    z = p[..., 2 * RT_QK_W + RT_V_W:]
    o_f, _ = _retention_scan(q, k, v, lg_f, sc_f, True)
    o_b, _ = _retention_scan(q[:, ::-1], k[:, ::-1], v[:, ::-1], lg_b, sc_b, False)
    y_lat = _retention_out(o_f + o_b[:, ::-1], z, w_out)
    if not ctx_out:
        return y_lat, None
    y_ctx = _retention_out(oc_f + oc_b[:, ::-1], pc[..., 2 * RT_QK_W + RT_V_W:], w_out)
    return y_lat, y_ctx


def _mixer_neighbourhood(h_lat, h_ctx, w_in, w_out, rpb, ctx_out):
    B, L, _ = h_lat.shape
    n_ctx = h_ctx.shape[1]
    rows = L // GRID_W
    kh = min(NA_KH, rows)
    scale = NA_HEAD_DIM ** -0.5
    p = h_lat @ w_in
    grid = (B, rows, GRID_W, NA_HEADS, NA_HEAD_DIM)
    k = p[..., :NA_W].reshape(grid)
    v = p[..., NA_W:2 * NA_W].reshape(grid)
    q = (p[..., 2 * NA_W:3 * NA_W] * scale).reshape(grid)
    z = p[..., 3 * NA_W:]
    pc = h_ctx @ (w_in if ctx_out else w_in[:, :2 * NA_W])
    kc = pc[..., :NA_W].reshape(B, n_ctx, NA_HEADS, NA_HEAD_DIM)
    vc = pc[..., NA_W:2 * NA_W].reshape(B, n_ctx, NA_HEADS, NA_HEAD_DIM)
    cq = jnp.arange(GRID_W)
    col_start = jnp.clip(cq - NA_KW // 2, 0, GRID_W - NA_KW)
    col_idx = col_start[:, None] + jnp.arange(NA_KW)[None, :]
    col_off = col_idx - cq[:, None] + (NA_KW - 1)
    rpb_f = rpb.astype(jnp.float32)

    def row_block(r):
        rs = jnp.clip(r - kh // 2, 0, rows - kh)
        k_g = lax.dynamic_slice_in_dim(k, rs, kh, axis=1)[:, :, col_idx]
        v_g = lax.dynamic_slice_in_dim(v, rs, kh, axis=1)[:, :, col_idx]
        q_r = lax.dynamic_index_in_dim(q, r, axis=1, keepdims=False)
        row_off = rs + jnp.arange(kh) - r + (NA_KH - 1)
        bias = jnp.transpose(rpb_f[:, row_off][:, :, col_off], (0, 2, 1, 3))
        s_loc = jnp.einsum('bqhd,baqwhd->bhqaw', q_r, k_g).astype(jnp.float32) + bias[None]
        s_loc = s_loc.reshape(B, NA_HEADS, GRID_W, kh * NA_KW)
        s_ctx = jnp.einsum('bqhd,bkhd->bhqk', q_r, kc).astype(jnp.float32)
        p_loc, p_ctx = _joint_softmax([s_loc, s_ctx])
        p_loc = p_loc.reshape(B, NA_HEADS, GRID_W, kh, NA_KW).astype(v_g.dtype)
        return (jnp.einsum('bhqaw,baqwhd->bqhd', p_loc, v_g)
                + jnp.einsum('bhqk,bkhd->bqhd', p_ctx.astype(vc.dtype), vc))

    o = lax.map(row_block, jnp.arange(rows))
    o = jnp.moveaxis(o, 0, 1).reshape(B, L, NA_W)
    y_lat = (o * jax.nn.silu(z)) @ w_out
    if not ctx_out:
        return y_lat, None
    qc = (pc[..., 2 * NA_W:3 * NA_W] * scale).reshape(B, n_ctx, NA_HEADS, NA_HEAD_DIM)
    (p_cc,) = _joint_softmax([jnp.einsum('bqhd,bkhd->bhqk', qc, kc).astype(jnp.float32)])
    oc = jnp.einsum('bhqk,bkhd->bqhd', p_cc.astype(vc.dtype), vc).reshape(B, n_ctx, NA_W)
    y_ctx = (oc * jax.nn.silu(pc[..., 3 * NA_W:])) @ w_out
    return y_lat, y_ctx


def _linear_combine(e1, e2):
    a1, b1 = e1
    a2, b2 = e2
    return a1 * a2, a2 * b1 + b2


def _s5_discretise(a_re, a_im, log_dt, b_mat):
    lam = lax.complex(a_re.astype(jnp.float32), a_im.astype(jnp.float32))
    dt = jnp.exp(log_dt.astype(jnp.float32))[:, None]
    lam_bar = jnp.exp(lam * dt)
    b_bar = ((lam_bar - 1.0) / lam)[..., None] * b_mat
    return lam_bar, b_bar


def _s5_scan(u, lam_bar, b_bar, c_mat, x0, emit):
    B, L, G, N = u.shape
    n = L // S5_CHUNK
    a = jnp.broadcast_to(lam_bar, (B, S5_CHUNK) + lam_bar.shape)

    def step(x, u_blk):
        bu = jnp.einsum('bcgn,gpn->bcgp', u_blk.astype(jnp.complex64), b_bar)
        bu = bu.at[:, 0].add(lam_bar * x)
        _, xs = lax.associative_scan(_linear_combine, (a, bu), axis=1)
        y = jnp.einsum('bcgp,gnp->bcgn', xs, c_mat).real if emit else None
        return xs[:, -1], y

    x_fin, y = lax.scan(step, x0, jnp.moveaxis(u.reshape(B, n, S5_CHUNK, G, N), 1, 0))
    if emit:
        y = jnp.moveaxis(y, 0, 1).reshape(B, L, G, N)
    return y, x_fin


def _s5_out(y, u, z, d_skip, w_glu, b_glu, w_out):
    B, L = y.shape[:2]
    y = (y + d_skip.astype(jnp.float32).reshape(S5_GROUPS, S5_GROUP) * u).reshape(B, L, S5_WIDTH)
    y = jax.nn.gelu(y).astype(z.dtype)
    y = y * jax.nn.sigmoid(y @ w_glu + b_glu)
    return (y * jax.nn.silu(z)) @ w_out


def _mixer_s5(h_lat, h_ctx, w_in, a_re, a_im, log_dt, b_re, b_im, c_re, c_im, d_skip, w_glu, b_glu, w_out, ctx_out):
    B, L, _ = h_lat.shape
    b_mat = lax.complex(b_re.astype(jnp.float32), b_im.astype(jnp.float32))
    c_mat = lax.complex(c_re.astype(jnp.float32), c_im.astype(jnp.float32))
    fwd = _s5_discretise(a_re[0], a_im[0], log_dt[0], b_mat)
    bwd = _s5_discretise(a_re[1], a_im[1], log_dt[1], b_mat)
    x0 = jnp.zeros((B, S5_GROUPS, S5_STATE), jnp.complex64)

    def groups(t):
        return t.reshape(B, t.shape[1], S5_GROUPS, S5_GROUP).astype(jnp.float32)

    pc = h_ctx @ (w_in if ctx_out else w_in[:, :S5_WIDTH])
    uc = groups(pc[..., :S5_WIDTH])
    yc_f, xc_f = _s5_scan(uc, fwd[0], fwd[1], c_mat, x0, ctx_out)
    yc_b, xc_b = _s5_scan(uc[:, ::-1], bwd[0], bwd[1], c_mat, x0, ctx_out)
    p = h_lat @ w_in
    u = groups(p[..., :S5_WIDTH])
    y_f, _ = _s5_scan(u, fwd[0], fwd[1], c_mat, xc_f, True)
    y_b, _ = _s5_scan(u[:, ::-1], bwd[0], bwd[1], c_mat, xc_b, True)
    y_lat = _s5_out(y_f + y_b[:, ::-1], u, p[..., S5_WIDTH:], d_skip, w_glu, b_glu, w_out)
    if not ctx_out:
        return y_lat, None
    y_ctx = _s5_out(yc_f + yc_b[:, ::-1], uc, pc[..., S5_WIDTH:], d_skip, w_glu, b_glu, w_out)
    return y_lat, y_ctx


def setup_inputs(seed: int = 0) -> dict:
    key = jax.random.key(seed)
    keys = iter(jax.random.split(key, 40))
    f32 = jnp.float32

    def nrm(shape, std=1.0):
        return std * jax.random.normal(next(keys), shape, f32)

    nA, nB, nC, nD = [len(range(m, DEPTH, N_MIXERS)) for m in range(N_MIXERS)]
    D = D_MODEL
    G, P, N, E = S5_GROUPS, S5_STATE, S5_GROUP, S5_WIDTH
    rt_base = jnp.log1p(-(2.0 ** (-5.0 - jnp.arange(RT_HEADS, dtype=f32))))
    return {
        'x': nrm((BATCH, SEQ, D)),
        'c': nrm((BATCH, D)),
        'ctx': nrm((BATCH, CTX_LEN, D)),
        'c_ctx': nrm((D,)),
        'mod_w': nrm((DEPTH, D, 3 * D), D ** -0.5),
        'mod_b': nrm((DEPTH, 3 * D), 0.01),
        'ln_g': 1.0 + nrm((DEPTH, D), 0.02),
        'ln_b': nrm((DEPTH, D), 0.02),
        'wa_w_in': nrm((nA, D, WA_IN_W), D ** -0.5),
        'wa_w_out': nrm((nA, WA_Q_W, D), DN_BETA * WA_Q_W ** -0.5),
        'wa_sink': nrm((nA, WA_HEADS), 0.5),
        'rt_w_in': nrm((nB, D, RT_IN_W), D ** -0.5),
        'rt_w_out': nrm((nB, RT_V_W, D), DN_BETA * RT_V_W ** -0.5),
        'rt_log_decay': rt_base * jnp.exp(nrm((nB, 2, RT_HEADS), 0.1)),
        'na_w_in': nrm((nC, D, NA_IN_W), D ** -0.5),
        'na_w_out': nrm((nC, NA_W, D), DN_BETA * NA_W ** -0.5),
        'na_rpb': nrm((nC, NA_HEADS, 2 * NA_KH - 1, 2 * NA_KW - 1), 0.1),
        's5_w_in': nrm((nD, D, S5_IN_W), D ** -0.5),
        's5_a_re': -0.5 * jnp.exp(nrm((nD, 2, G, P), 0.05)),
        's5_a_im': jnp.pi * jnp.arange(P, dtype=f32) + nrm((nD, 2, G, P), 0.05),
        's5_log_dt': jax.random.uniform(next(keys), (nD, 2, G), f32, math.log(1e-3), math.log(1e-1)),
        's5_b_re': nrm((nD, G, P, N), (2 * N) ** -0.5),
        's5_b_im': nrm((nD, G, P, N), (2 * N) ** -0.5),
        's5_c_re': nrm((nD, G, N, P), (2 * P) ** -0.5),
        's5_c_im': nrm((nD, G, N, P), (2 * P) ** -0.5),
        's5_d': nrm((nD, E)),
        's5_w_glu': nrm((nD, E, E), E ** -0.5),
        's5_b_glu': nrm((nD, E), 0.01),
        's5_w_out': nrm((nD, E, D), DN_BETA * E ** -0.5),
    }


def reference(x, c, ctx, c_ctx, mod_w, mod_b, ln_g, ln_b,
              wa_w_in, wa_w_out, wa_sink,
              rt_w_in, rt_w_out, rt_log_decay,
              na_w_in, na_w_out, na_rpb,
              s5_w_in, s5_a_re, s5_a_im, s5_log_dt, s5_b_re, s5_b_im, s5_c_re, s5_c_im,
              s5_d, s5_w_glu, s5_b_glu, s5_w_out):
    Dm = D_MODEL
    silu_c = jax.nn.silu(c)
    silu_cc = jax.nn.silu(c_ctx)
    x_ctx = ctx
    for i in range(DEPTH):
        m, j = i % N_MIXERS, i // N_MIXERS
        ctx_out = i < DEPTH - 1
        mod = silu_c @ mod_w[i] + mod_b[i]
        shift, scale, gate = jnp.split(mod[:, None, :], 3, axis=-1)
        n_mod = (3 if ctx_out else 2) * Dm
        mod_c = silu_cc @ mod_w[i][:, :n_mod] + mod_b[i][:n_mod]
        h_lat = x * (1.0 + scale) + shift
        h_ctx = x_ctx * (1.0 + mod_c[Dm:2 * Dm]) + mod_c[:Dm]
        if m == 0:
            y_lat, y_ctx = _mixer_window_gqa(h_lat, h_ctx, wa_w_in[j], wa_w_out[j], wa_sink[j], ctx_out)
        elif m == 1:
            y_lat, y_ctx = _mixer_retention(h_lat, h_ctx, rt_w_in[j], rt_w_out[j], rt_log_decay[j], ctx_out)
        elif m == 2:
            y_lat, y_ctx = _mixer_neighbourhood(h_lat, h_ctx, na_w_in[j], na_w_out[j], na_rpb[j], ctx_out)
        else:
            y_lat, y_ctx = _mixer_s5(h_lat, h_ctx, s5_w_in[j], s5_a_re[j], s5_a_im[j], s5_log_dt[j],
                                     s5_b_re[j], s5_b_im[j], s5_c_re[j], s5_c_im[j], s5_d[j],
                                     s5_w_glu[j], s5_b_glu[j], s5_w_out[j], ctx_out)
        x = _layer_norm(DN_ALPHA * x + gate * y_lat, ln_g[i], ln_b[i])
        if ctx_out:
            x_ctx = _layer_norm(DN_ALPHA * x_ctx + mod_c[2 * Dm:] * y_ctx, ln_g[i], ln_b[i])
    return x
```

```python
import contextlib, math
import numpy as np
import ml_dtypes
from concourse.bass_utils import run_bass_kernel_spmd
import concourse.bass as bass
import concourse.mybir as mybir

F32, BF16 = mybir.dt.float32, mybir.dt.bfloat16
AF = mybir.ActivationFunctionType
ALU = mybir.AluOpType
AX = mybir.AxisListType
D = 4096
KT = 32
CTX = 256
LN_EPS = 1e-6
DN_ALPHA = 8.0 ** 0.25
DEBUG = False


class Buf:
    __slots__ = ("name", "w", "r", "ch", "par")

    def __init__(s, name, par=None):
        s.name = name; s.w = None; s.r = {}; s.ch = {}; s.par = par

    def sub(s, k):
        c = s.ch.get(k)
        if c is None:
            c = s.ch[k] = Buf(f"{s.name}.{k}", s)
        return c

    def desc(s):
        for c in s.ch.values():
            yield c
            yield from c.desc()

    def anc(s):
        p = s.par
        while p is not None:
            yield p
            p = p.par


class KB:
    def __init__(s, nc, es, nslot=40):
        s.nc = nc; s.es = es
        s.eng = {"pe": nc.tensor, "dve": nc.vector, "act": nc.scalar, "pool": nc.gpsimd, "sp": nc.sync}
        s.sem = {e: es.enter_context(nc.semaphore(f"sem_{e}")) for e in s.eng}
        s.cnt = {e: 0 for e in s.eng}
        s.waited = {e: {} for e in s.eng}
        s.dsem = [es.enter_context(nc.semaphore(f"dsem{i}")) for i in range(nslot)]
        s.duse = [0] * nslot
        s.dnext = 0
        s.nbuf = 0

    def buf(s, name=None):
        s.nbuf += 1
        return Buf(name or f"b{s.nbuf}")

    def sb(s, shape, dt, name=None):
        s.nbuf += 1
        name = name or f"sb{s.nbuf}"
        t = s.es.enter_context(s.nc.sbuf_tensor(name, list(shape), dt))
        return t, Buf(name)

    def ps(s, shape, dt, name=None):
        s.nbuf += 1
        name = name or f"ps{s.nbuf}"
        t = s.es.enter_context(s.nc.psum_tensor(name, list(shape), dt))
        return t, Buf(name)

    def dram(s, shape=None, dt=None, name=None):
        s.nbuf += 1
        name = name or f"dr{s.nbuf}"
        t = s.nc.dram_tensor(name, list(shape), dt, kind=("ExternalOutput" if DEBUG else "Internal")).ap()
        return t, Buf(name)

    def _wait(s, e, ev):
        sem, val = ev
        k = id(sem)
        if s.waited[e].get(k, 0) >= val:
            return
        s.eng[e].wait_ge(sem, val)
        s.waited[e][k] = val

    def _deps(s, reads, writes):
        evs = {}

        def add(ev):
            k = id(ev[0])
            if k not in evs or evs[k][1] < ev[1]:
                evs[k] = ev
        for b in reads:
            for x in (b, *b.anc(), *b.desc()):
                if x.w:
                    add(x.w)
        for b in writes:
            for x in (b, *b.anc(), *b.desc()):
                if x.w:
                    add(x.w)
                for ev in x.r.values():
                    add(ev)
        return list(evs.values())

    def _record(s, ev, reads, writes):
        for b in writes:
            b.w = ev; b.r = {}
            for x in b.desc():
                x.w = None; x.r = {}
        for b in reads:
            b.r[id(ev[0])] = ev

    def op(s, e, fn, reads=(), writes=()):
        for ev in s._deps(reads, writes):
            if e == "pe" and ev[0] is s.sem["pe"]:
                continue
            s._wait(e, ev)
        ins = fn(s.eng[e])
        s.cnt[e] += 1
        ins.then_inc(s.sem[e], 1)
        s._record((s.sem[e], s.cnt[e]), reads, writes)

    def dma(s, q, out, in_, reads=(), writes=(), **kw):
        evs = s._deps(reads, writes)
        i = s.dnext
        s.dnext = (s.dnext + 1) % len(s.dsem)
        if s.duse[i] > 0:
            evs.append((s.dsem[i], 16 * s.duse[i]))
        for ev in evs:
            s._wait(q, ev)
        ins = s.eng[q].dma_start(out=out, in_=in_, **kw)
        s.duse[i] += 1
        ins.then_inc(s.dsem[i], 16)
        s._record((s.dsem[i], 16 * s.duse[i]), reads, writes)

    def barrier(s):
        for e in s.eng:
            for i, u in enumerate(s.duse):
                if u:
                    s._wait(e, (s.dsem[i], 16 * u))
            for e2 in ("pe", "dve", "act", "pool"):
                if s.cnt[e2] and not (e == e2 == "pe"):
                    s._wait(e, (s.sem[e2], s.cnt[e2]))

    def finish(s):
        for i, u in enumerate(s.duse):
            if u:
                s._wait("sp", (s.dsem[i], 16 * u))
        for e in ("pe", "dve", "act", "pool"):
            if s.cnt[e]:
                s._wait("sp", (s.sem[e], s.cnt[e]))


class Rot:
    def __init__(s, items):
        s.items = items; s.i = 0

    def next(s):
        it = s.items[s.i]
        s.i = (s.i + 1) % len(s.items)
        return it


def gemm(kb, es_outer, actT_d, actT_b, K, toks, w_d, coltiles, mode, evac, ncol=128, tp_max=1280, psum_pool=None, wq="pool"):
    nc = kb.nc
    kt_n = K // 128
    t_lo, t_hi = toks
    for p0 in range(t_lo, t_hi, tp_max):
        p1 = min(t_hi, p0 + tp_max)
        tp = p1 - p0
        with phase(kb):
            A, Ab = kb.sb([128, kt_n, tp], BF16)
            kb.dma("sp", A[:], actT_d[:, p0:p1].rearrange("(kt p) t -> p kt t", p=128), reads=[actT_b], writes=[Ab])
            W = Rot([kb.sb([128, kt_n, ncol], BF16) for _ in range(3)])
            P = psum_pool
            for c0 in coltiles:
                wt, wb = W.next()
                kb.dma(wq, wt[:], w_d[:, c0:c0 + ncol].rearrange("(kt p) c -> p kt c", p=128), writes=[wb])
                if mode == "fm":
                    for b0 in range(0, tp, 512):
                        n = min(512, tp - b0)
                        pt, pb = P.next()
                        for kt in range(kt_n):
                            kb.op("pe", lambda e, kt=kt: e.matmul(pt[:, 0:n], lhsT=wt[:, kt, :], rhs=A[:, kt, b0:b0 + n], start=(kt == 0), stop=(kt == kt_n - 1)),
                                  reads=[wb, Ab], writes=[pb])
                        evac(pt[:, 0:n], pb, c0, p0 + b0, n)
                else:
                    for b0 in range(0, tp, 128):
                        pt, pb = P.next()
                        for kt in range(kt_n):
                            kb.op("pe", lambda e, kt=kt: e.matmul(pt[:, 0:ncol], lhsT=A[:, kt, b0:b0 + 128], rhs=wt[:, kt, :], start=(kt == 0), stop=(kt == kt_n - 1)),
                                  reads=[wb, Ab], writes=[pb])
                        evac(pt[:, 0:ncol], pb, c0, p0 + b0)


@contextlib.contextmanager
def phase(kb):
    old = kb.es
    with contextlib.ExitStack() as es:
        kb.es = es
        yield es
        kb.barrier()
    kb.es = old


def mod_phase(kb, c_d, cc_d, modw_d, modb_d, modrow_d, modrow_b, P):
    with phase(kb):
        craw, crb = kb.sb([128, KT, 2], F32)
        sc, scb = kb.sb([128, KT, 2], F32)
        kb.dma("sp", craw[:, :, 0], c_d.rearrange("(kt p) -> p kt", p=128), writes=[crb.sub(0)], allow_slow_non_contiguous=True)
        kb.dma("sp", craw[:, :, 1], cc_d.rearrange("(kt p) -> p kt", p=128), writes=[crb.sub(1)], allow_slow_non_contiguous=True)
        kb.op("act", lambda e: e.activation(out=sc[:], in_=craw[:], func=AF.Silu), reads=[crb], writes=[scb])
        bt, btb = kb.sb([2, 12288], F32)
        ot, otb = kb.sb([2, 12288], F32)
        mb2 = modb_d.rearrange("(o n) -> o n", o=1)
        kb.dma("sp", bt[0:1, :], mb2, writes=[btb.sub(0)])
        kb.dma("sp", bt[1:2, :], mb2, writes=[btb.sub(1)])
        W = Rot([kb.sb([128, KT, 256], F32) for _ in range(2)])
        for nb in range(48):
            wt, wb = W.next()
            kb.dma("sp", wt[:], modw_d[:, nb * 256:(nb + 1) * 256].rearrange("(kt p) c -> p kt c", p=128), writes=[wb])
            pt, pb = P.next()
            for kt in range(KT):
                kb.op("pe", lambda e, kt=kt: e.matmul(pt[0:2, 0:256], lhsT=sc[:, kt, :], rhs=wt[:, kt, :], start=(kt == 0), stop=(kt == KT - 1)),
                      reads=[scb, wb], writes=[pb])
            kb.op("dve", lambda e: e.tensor_tensor(out=ot[:, nb * 256:(nb + 1) * 256], in0=pt[0:2, 0:256], in1=bt[:, nb * 256:(nb + 1) * 256], op=ALU.add),
                  reads=[pb, btb], writes=[otb.sub(nb)])
        kb.dma("sp", modrow_d[:, :], ot[:], reads=[otb], writes=[modrow_b])


def prep(kb, x_d, x_b, ntiles, modrow_d, modrow_b, r, hT_d, hT_b, col0, ident, identb, P):
    with phase(kb):
        sh, shb = kb.sb([128, KT], F32)
        s1, s1b = kb.sb([128, KT], F32)
        kb.dma("sp", sh[:], modrow_d[r, 0:D].rearrange("(kt p) -> p kt", p=128), reads=[modrow_b], writes=[shb], allow_slow_non_contiguous=True)
        kb.dma("sp", s1[:], modrow_d[r, D:2 * D].rearrange("(kt p) -> p kt", p=128), reads=[modrow_b], writes=[s1b], allow_slow_non_contiguous=True)
        kb.op("dve", lambda e: e.tensor_scalar(out=s1[:], in0=s1[:], scalar1=1.0, scalar2=None, op0=ALU.add), reads=[s1b], writes=[s1b])
        X = Rot([kb.sb([128, D], F32) for _ in range(2)])
        H = Rot([kb.sb([128, KT, 128], BF16) for _ in range(2)])
        for tt in range(ntiles):
            xt, xb = X.next()
            ht, hb = H.next()
            kb.dma("sp", xt[:], x_d[tt * 128:(tt + 1) * 128, :], reads=[x_b], writes=[xb])
            for kt in range(KT):
                pt, pb = P.next()
                kb.op("pe", lambda e, kt=kt: e.transpose(out=pt[:, 0:128], in_=xt[:, kt * 128:(kt + 1) * 128], identity=ident[:]),
                      reads=[xb, identb], writes=[pb])
                kb.op("act", lambda e, kt=kt: e.activation(out=ht[:, kt, :], in_=pt[:, 0:128], func=AF.Identity, bias=sh[:, kt:kt + 1], scale=s1[:, kt:kt + 1]),
                      reads=[pb, shb, s1b], writes=[hb.sub(kt)])
            c = col0 + tt * 128
            kb.dma("sp", hT_d[:, c:c + 128].rearrange("(kt p) t -> p kt t", p=128), ht[:], reads=[hb], writes=[hT_b.sub(c // 128)])


def epilogue(kb, x_d, x_b, ys, ntiles, yrow0, modrow_d, modrow_b, r, lng_d, lnb_d, out_d, out_b):
    with phase(kb):
        gate, gateb = kb.sb([128, D], F32)
        lg, lgb = kb.sb([128, D], F32)
        lb, lbb = kb.sb([128, D], F32)
        kb.dma("sp", gate[:], modrow_d[r, 2 * D:3 * D].partition_broadcast(128), reads=[modrow_b], writes=[gateb])
        kb.dma("sp", lg[:], lng_d.partition_broadcast(128), writes=[lgb])
        kb.dma("sp", lb[:], lnb_d.partition_broadcast(128), writes=[lbb])
        X = Rot([kb.sb([128, D], F32) for _ in range(2)])
        Y = Rot([kb.sb([128, D], F32) for _ in range(2)])
        Y2 = Rot([kb.sb([128, D], F32) for _ in range(1)]) if len(ys) > 1 else None
        ST = Rot([kb.sb([128, 8, 6], F32) for _ in range(2)])
        MV = Rot([kb.sb([128, 4], F32) for _ in range(2)])
        for tt in range(ntiles):
            xt, xb = X.next()
            yt, yb = Y.next()
            st, stb = ST.next()
            mv, mvb = MV.next()
            kb.dma("sp", xt[:], x_d[tt * 128:(tt + 1) * 128, :], reads=[x_b], writes=[xb])
            y_d, y_b = ys[0]
            kb.dma("sp", yt[:], y_d[yrow0 + tt * 128:yrow0 + (tt + 1) * 128, :], reads=[y_b], writes=[yb])
            for (y_d2, y_b2) in ys[1:]:
                y2, y2b = Y2.next()
                kb.dma("sp", y2[:], y_d2[yrow0 + tt * 128:yrow0 + (tt + 1) * 128, :], reads=[y_b2], writes=[y2b])
                kb.op("pool", lambda e: e.tensor_tensor(out=yt[:], in0=yt[:], in1=y2[:], op=ALU.add), reads=[yb, y2b], writes=[yb])
            kb.op("dve", lambda e: e.tensor_tensor(out=yt[:], in0=yt[:], in1=gate[:], op=ALU.mult), reads=[yb, gateb], writes=[yb])
            kb.op("dve", lambda e: e.scalar_tensor_tensor(out=xt[:], in0=xt[:], scalar=DN_ALPHA, in1=yt[:], op0=ALU.mult, op1=ALU.add), reads=[xb, yb], writes=[xb])
            for c in range(8):
                kb.op("dve", lambda e, c=c: e.bn_stats(out=st[:, c, :], in_=xt[:, c * 512:(c + 1) * 512]), reads=[xb], writes=[stb.sub(c)])
            kb.op("dve", lambda e: e.bn_aggr(out=mv[:, 0:2], in_=st[:].rearrange("p a b -> p (a b)")), reads=[stb], writes=[mvb])
            kb.op("dve", lambda e: e.tensor_scalar(out=mv[:, 3:4], in0=mv[:, 1:2], scalar1=LN_EPS, scalar2=None, op0=ALU.add), reads=[mvb], writes=[mvb])
            kb.op("act", lambda e: e.activation(out=mv[:, 3:4], in_=mv[:, 3:4], func=AF.Sqrt), reads=[mvb], writes=[mvb])
            kb.op("dve", lambda e: e.reciprocal(out=mv[:, 2:3], in_=mv[:, 3:4]), reads=[mvb], writes=[mvb])
            kb.op("dve", lambda e: e.tensor_scalar(out=xt[:], in0=xt[:], scalar1=mv[:, 0:1], scalar2=mv[:, 2:3], op0=ALU.subtract, op1=ALU.mult), reads=[xb, mvb], writes=[xb])
            kb.op("pool", lambda e: e.tensor_tensor(out=xt[:], in0=xt[:], in1=lg[:], op=ALU.mult), reads=[xb, lgb], writes=[xb])
            kb.op("pool", lambda e: e.tensor_tensor(out=xt[:], in0=xt[:], in1=lb[:], op=ALU.add), reads=[xb, lbb], writes=[xb])
            kb.dma("sp", out_d[tt * 128:(tt + 1) * 128, :], xt[:], reads=[xb], writes=[out_b.sub(tt)])


def outproj(kb, ogT_d, ogT_b, K, ntok, w_d, y_d, y_b, P, k0=0):
    with phase(kb):
        YS = Rot([kb.sb([128, 256], F32) for _ in range(3)])

        def evac(ps, pb, c0, t0):
            ys, ysb = YS.next()
            kb.op("act", lambda e: e.copy(out=ys[:], in_=ps), reads=[pb], writes=[ysb])
            kb.dma("sp", y_d[t0:t0 + 128, c0:c0 + 256], ys[:], reads=[ysb], writes=[y_b.sub((t0, c0))])
        gemm(kb, None, ogT_d[k0:k0 + K, :], ogT_b, K, (0, ntok), w_d[k0:k0 + K, :], list(range(0, D, 256)), "tm", evac, ncol=256, psum_pool=P)


def proj_fm(kb, hT_d, hT_b, toks, w_d, col0, ncols, out_d, out_b, out_dt, P, func=None):
    with phase(kb):
        t_lo, t_hi = toks
        OS = Rot([kb.sb([128, 512], out_dt) for _ in range(3)])
        cnt = [0]

        def evac(ps, pb, c0, t0, n):
            os_, osb = OS.next()
            cnt[0] += 1
            if func is not None:
                kb.op("act", lambda e: e.activation(out=os_[:, 0:n], in_=ps, func=func), reads=[pb], writes=[osb])
            elif cnt[0] % 2:
                kb.op("act", lambda e: e.copy(out=os_[:, 0:n], in_=ps), reads=[pb], writes=[osb])
            else:
                kb.op("dve", lambda e: e.tensor_copy(out=os_[:, 0:n], in_=ps), reads=[pb], writes=[osb])
            r0 = c0 - col0
            kb.dma("sp", out_d[r0:r0 + 128, t0:t0 + n], os_[:, 0:n], reads=[osb], writes=[out_b.sub((r0, t0))])
        gemm(kb, None, hT_d, hT_b, D, toks, w_d, list(range(col0, col0 + ncols, 128)), "fm", evac, psum_pool=P)


def proj_tm(kb, hT_d, hT_b, toks, w_d, col0, ncols, out_d, out_b, out_dt, P, func=None):
    with phase(kb):
        OS = Rot([kb.sb([128, 256], out_dt) for _ in range(3)])

        def evac(ps, pb, c0, t0):
            os_, osb = OS.next()
            if func is not None:
                kb.op("act", lambda e: e.activation(out=os_[:], in_=ps, func=func), reads=[pb], writes=[osb])
            else:
                kb.op("dve", lambda e: e.tensor_copy(out=os_[:], in_=ps), reads=[pb], writes=[osb])
            r0 = c0 - col0
            kb.dma("sp", out_d[t0:t0 + 128, r0:r0 + 256], os_[:], reads=[osb], writes=[out_b.sub((t0, r0))])
        gemm(kb, None, hT_d, hT_b, D, toks, w_d, list(range(col0, col0 + ncols, 256)), "tm", evac, ncol=256, psum_pool=P)


class Attn:
    def __init__(s, kb, P, PB, identb_t, identb_b, nk_max=640):
        s.kb = kb; s.P = P; s.PB = PB
        s.ident = identb_t; s.identb = identb_b
        s.S = Rot([kb.sb([128, nk_max], F32) for _ in range(2)])
        s.Pb = Rot([kb.sb([128, nk_max], BF16) for _ in range(2)])
        s.PT = Rot([kb.sb([128, nk_max // 128, 128], BF16) for _ in range(2)])
        s.ST = Rot([kb.sb([128, 8], F32) for _ in range(2)])

    def run(s, qT, qb, segs, vtiles, sink, sinkb, zT, zb, og, ogb, scale):
        kb = s.kb
        S, Sb = s.S.next(); Pb, Pbb = s.Pb.next(); PT, PTb = s.PT.next(); st, stb = s.ST.next()
        off = 0
        for i, seg in enumerate(segs):
            kT, kbuf, mask, mbuf = seg[:4]
            n = kT.shape[1]
            pt, pb = s.P.next()
            kb.op("pe", lambda e: e.matmul(pt[:, 0:n], lhsT=qT, rhs=kT, start=True, stop=True), reads=[qb, kbuf], writes=[pb])
            if mask is not None:
                kb.op("dve", lambda e: e.scalar_tensor_tensor(out=S[:, off:off + n], in0=pt[:, 0:n], scalar=scale, in1=mask, op0=ALU.mult, op1=ALU.add),
                      reads=[pb, mbuf], writes=[Sb.sub(i)])
                if len(seg) > 4:
                    kb.op("pool", lambda e: e.tensor_tensor(out=S[:, off:off + n], in0=S[:, off:off + n], in1=seg[4], op=ALU.add),
                          reads=[Sb.sub(i), seg[5]], writes=[Sb.sub(i)])
            else:
                kb.op("act", lambda e: e.activation(out=S[:, off:off + n], in_=pt[:, 0:n], func=AF.Copy, scale=scale), reads=[pb], writes=[Sb.sub(i)])
            off += n
        nk = off
        kb.op("dve", lambda e: e.reduce_max(out=st[:, 0:1], in_=S[:, 0:nk], axis=AX.X), reads=[Sb], writes=[stb])
        if sink is not None:
            kb.op("dve", lambda e: e.tensor_scalar(out=st[:, 1:2], in0=st[:, 0:1], scalar1=sink, scalar2=-1.0, op0=ALU.max, op1=ALU.mult), reads=[stb, sinkb], writes=[stb])
        else:
            kb.op("dve", lambda e: e.tensor_scalar(out=st[:, 1:2], in0=st[:, 0:1], scalar1=-1.0, scalar2=None, op0=ALU.mult), reads=[stb], writes=[stb])
        kb.op("act", lambda e: e.activation(out=S[:, 0:nk], in_=S[:, 0:nk], func=AF.Exp, bias=st[:, 1:2], scale=1.0, accum_out=st[:, 2:3]), reads=[Sb, stb], writes=[Sb, stb])
        if sink is not None:
            kb.op("act", lambda e: e.activation(out=st[:, 3:4], in_=st[:, 1:2], func=AF.Exp, bias=sink, scale=1.0), reads=[stb, sinkb], writes=[stb])
            kb.op("dve", lambda e: e.tensor_tensor(out=st[:, 2:3], in0=st[:, 2:3], in1=st[:, 3:4], op=ALU.add), reads=[stb], writes=[stb])
        kb.op("dve", lambda e: e.reciprocal(out=st[:, 4:5], in_=st[:, 2:3]), reads=[stb], writes=[stb])
        kb.op("dve", lambda e: e.tensor_scalar(out=Pb[:, 0:nk], in0=S[:, 0:nk], scalar1=st[:, 4:5], scalar2=None, op0=ALU.mult), reads=[Sb, stb], writes=[Pbb])
        nt = nk // 128
        for g0 in range(0, nt, 8):
            g1 = min(nt, g0 + 8)
            ptb, ptbb = s.PB.next()
            for kt in range(g0, g1):
                kb.op("pe", lambda e, kt=kt: e.transpose(out=ptb[:, (kt - g0) * 128:(kt - g0 + 1) * 128], in_=Pb[:, kt * 128:(kt + 1) * 128], identity=s.ident[:]),
                      reads=[Pbb, s.identb], writes=[ptbb])
            kb.op("act", lambda e: e.copy(out=PT[:, g0:g1, :], in_=ptb[:, 0:(g1 - g0) * 128].rearrange("p (a b) -> p a b", b=128)), reads=[ptbb], writes=[PTb.sub(g0)])
        po, pob = s.P.next()
        for kt in range(nt):
            v, vb = vtiles[kt]
            kb.op("pe", lambda e, kt=kt, v=v: e.matmul(po[:, 0:128], lhsT=v, rhs=PT[:, kt, :], start=(kt == 0), stop=(kt == nt - 1)), reads=[vb, PTb], writes=[pob])
        kb.op("dve", lambda e: e.tensor_tensor(out=og, in0=po[:, 0:128], in1=zT, op=ALU.mult), reads=[pob, zb], writes=[ogb])


def rope_fm(kb, P, raw, rawb, n, cos, sin, csb, permT, permb, out, outb, TMP):
    for b0 in range(0, n, 512):
        m = min(512, n - b0)
        (tb, tbb), (t1, t1b), (t2, t2b) = TMP.next()
        kb.op("act", lambda e: e.copy(out=tb[:, 0:m], in_=raw[:, b0:b0 + m]), reads=[rawb], writes=[tbb])
        pt, pb = P.next()
        kb.op("pe", lambda e: e.matmul(pt[:, 0:m], lhsT=permT[:], rhs=tb[:, 0:m], start=True, stop=True), reads=[tbb, permb], writes=[pb])
        kb.op("dve", lambda e: e.tensor_tensor(out=t1[:, 0:m], in0=raw[:, b0:b0 + m], in1=cos[:, b0:b0 + m], op=ALU.mult), reads=[rawb, csb], writes=[t1b])
        kb.op("dve", lambda e: e.tensor_tensor(out=t2[:, 0:m], in0=pt[:, 0:m], in1=sin[:, b0:b0 + m], op=ALU.mult), reads=[pb, csb], writes=[t2b])
        kb.op("pool", lambda e: e.tensor_tensor(out=out[:, b0:b0 + m], in0=t1[:, 0:m], in1=t2[:, 0:m], op=ALU.add), reads=[t1b, t2b], writes=[outb.sub(b0)])


def wa_mixer(kb, LT, kT_d, kT_b, qT_d, qT_b, zT_d, zT_b, v_d, v_b, ogT_d, ogT_b, cos_d, sin_d, perm_d, mask_d, sink_d, identb_d, P, PB, ctx_out=True):
    NB = LT // 128
    LH = LT + 256
    TT = LT + 512
    scale = 128 ** -0.5
    with phase(kb):
        cos, csb = kb.sb([128, LH], F32)
        sin, _ = kb.sb([128, LH], F32)
        kb.dma("sp", cos[:], cos_d[:, :], writes=[csb.sub(0)])
        kb.dma("sp", sin[:], sin_d[:, :], writes=[csb.sub(1)])
        permT, permb = kb.sb([128, 128], BF16)
        kb.dma("sp", permT[:], perm_d[:, :], writes=[permb])
        identb, identbb = kb.sb([128, 128], BF16)
        kb.dma("sp", identb[:], identb_d[:, :], writes=[identbb])
        mask, maskb = kb.sb([128, 3, 384], F32)
        kb.dma("sp", mask[:], mask_d.rearrange("c p n -> p c n"), writes=[maskb])
        sink, sinkb = kb.sb([128, 32], F32)
        kb.dma("sp", sink[:], sink_d.partition_broadcast(128), writes=[sinkb])
        TMP = Rot([(kb.sb([128, 512], BF16), kb.sb([128, 512], F32), kb.sb([128, 512], F32)) for _ in range(2)])
        TMP.items = [((a[0], a[1]), (b[0], b[1]), (c[0], c[1])) for (a, b, c) in TMP.items]
        A = Attn(kb, P, PB, identb, identbb)
        kraw, krawb = kb.sb([128, TT], F32)
        K, Kb = kb.sb([128, TT], BF16)
        V, Vb = kb.sb([128, TT // 128, 128], BF16)
        qraw, qrawb = kb.sb([128, TT], F32)
        Q, Qb = kb.sb([128, LT + 256], BF16)
        Z, Zb = kb.sb([128, TT], BF16)
        OG, OGb = kb.sb([128, LT + 256], BF16)
        for g in range(8):
            kb.dma("sp", kraw[:], kT_d[g * 128:(g + 1) * 128, :], reads=[kT_b], writes=[krawb])
            rope_fm(kb, P, kraw, krawb, LH, cos, sin, csb, permT, permb, K, Kb, TMP)
            kb.op("act", lambda e: e.copy(out=K[:, LH:TT], in_=kraw[:, LH:TT]), reads=[krawb], writes=[Kb.sub("ctx")])
            kb.dma("sp", V[:], v_d[:, g * 128:(g + 1) * 128].rearrange("(n p) d -> p n d", p=128), reads=[v_b], writes=[Vb])
            for h in range(4 * g, 4 * g + 4):
                kb.dma("sp", qraw[:], qT_d[h * 128:(h + 1) * 128, :], reads=[qT_b], writes=[qrawb])
                kb.dma("sp", Z[:], zT_d[h * 128:(h + 1) * 128, :], reads=[zT_b], writes=[Zb])
                rope_fm(kb, P, qraw[:, 128:128 + LT], qrawb, LT, cos[:, 128:128 + LT], sin[:, 128:128 + LT], csb, permT, permb, Q, Qb, TMP)
                kb.op("act", lambda e: e.copy(out=Q[:, LT:LT + 256], in_=qraw[:, LH:TT]), reads=[qrawb], writes=[Qb.sub("ctx")])
                for j in range(NB):
                    cls = 0 if j == 0 else (2 if j == NB - 1 else 1)
                    segs = [(K[:, j * 128:j * 128 + 384], Kb, mask[:, cls, :], maskb), (K[:, LH:TT], Kb, None, None)]
                    vt = [(V[:, j + i, :], Vb) for i in range(3)] + [(V[:, NB + 2 + i, :], Vb) for i in range(2)]
                    A.run(Q[:, j * 128:(j + 1) * 128], Qb, segs, vt, sink[:, h:h + 1], sinkb, Z[:, 128 + j * 128:128 + (j + 1) * 128], Zb,
                          OG[:, j * 128:(j + 1) * 128], OGb.sub(j), scale)
                if ctx_out:
                    for j in range(2):
                        segs = [(K[:, LH:TT], Kb, None, None)]
                        vt = [(V[:, NB + 2 + i, :], Vb) for i in range(2)]
                        A.run(Q[:, LT + j * 128:LT + (j + 1) * 128], Qb, segs, vt, sink[:, h:h + 1], sinkb, Z[:, LH + j * 128:LH + (j + 1) * 128], Zb,
                              OG[:, LT + j * 128:LT + (j + 1) * 128], OGb.sub(NB + j), scale)
                kb.dma("sp", ogT_d[h * 128:(h + 1) * 128, :], OG[:], reads=[OGb], writes=[ogT_b.sub(h)])


def wa_consts(ncores, LT, core):
    LH = LT + 256
    L = ncores * LT
    pos = core * LT - 128 + np.arange(LH)
    rows, cols = pos // 64, pos % 64
    invf = 10000.0 ** (-np.arange(32, dtype=np.float32) / 32)
    cos = np.zeros((128, LH), np.float32); sin = np.zeros((128, LH), np.float32)
    for dd in range(128):
        j = dd % 32
        p = rows if dd < 64 else cols
        ang = p.astype(np.float32) * invf[j]
        sgn = -1.0 if (dd % 64) < 32 else 1.0
        cos[dd] = np.cos(ang); sin[dd] = sgn * np.sin(ang)
    perm = np.zeros((128, 128), np.float32)
    for dd in range(128):
        partner = dd + 32 if (dd % 64) < 32 else dd - 32
        perm[partner, dd] = 1.0
    q = np.arange(128)[:, None]; k = np.arange(128)[None, :]
    NEG = -1e30
    left = np.where(k >= q, 0.0, NEG).astype(np.float32)
    right = np.where(k <= q, 0.0, NEG).astype(np.float32)
    mid = np.zeros((128, 128), np.float32)
    allneg = np.full((128, 128), NEG, np.float32)
    m_mid = np.concatenate([left, mid, right], 1)
    m_first = np.concatenate([allneg if core == 0 else left, mid, right], 1)
    m_last = np.concatenate([left, mid, allneg if core == ncores - 1 else right], 1)
    mask = np.stack([m_first, m_mid, m_last]).astype(np.float32)
    return dict(cos=cos, sin=sin, perm=perm.astype(ml_dtypes.bfloat16), mask=mask,
                ident32=np.eye(128, dtype=np.float32), identb=np.eye(128).astype(ml_dtypes.bfloat16))


def build_layer0(LT):
    LH = LT + 256; TT = LT + 512
    nc = bass.Bass("TRN2", target_bir_lowering=False)

    def inp(name, shape, dt=F32):
        return nc.dram_tensor(name, list(shape), dt, kind="ExternalInput").ap()
    xs = inp("xs", [LH, D]); ctx = inp("ctx", [CTX, D]); c = inp("c", [D]); cc = inp("cc", [D])
    modw = inp("modw", [D, 3 * D]); modb = inp("modb", [3 * D]); lng = inp("lng", [D]); lnb = inp("lnb", [D])
    w_in = inp("w_in", [D, 10240]); w_out = inp("w_out", [D, D]); sink = inp("sink", [32])
    cos = inp("cos", [128, LH]); sin = inp("sin", [128, LH]); perm = inp("perm", [128, 128], BF16)
    mask = inp("mask", [3, 128, 384]); ident32 = inp("ident32", [128, 128]); identb = inp("identb", [128, 128], BF16)
    xo = nc.dram_tensor("xo", [LT, D], F32, kind="ExternalOutput").ap()
    co = nc.dram_tensor("co", [CTX, D], F32, kind="ExternalOutput").ap()
    with contextlib.ExitStack() as es:
        kb = KB(nc, es)
        P = Rot([kb.ps([128, 512], F32) for _ in range(6)])
        PB = Rot([kb.ps([128, 1024], BF16) for _ in range(2)])
        ident, identb32 = kb.sb([128, 128], F32)
        kb.dma("sp", ident[:], ident32[:, :], writes=[identb32])
        ext = kb.buf("ext")
        modrow_d, modrow_b = kb.dram([2, 3 * D], F32, name="dbg_modrow")
        hT_d, hT_b = kb.dram([D, TT], BF16, name="dbg_hT")
        kT_d, kT_b = kb.dram([1024, TT], F32, name="dbg_kT")
        qT_d, qT_b = kb.dram([D, TT], F32, name="dbg_qT")
        zT_d, zT_b = kb.dram([D, TT], BF16, name="dbg_zT")
        v_d, v_b = kb.dram([TT, 1024], BF16, name="dbg_v")
        ogT_d, ogT_b = kb.dram([D, LT + CTX], BF16, name="dbg_ogT")
        y_d, y_b = kb.dram([LT + CTX, D], F32, name="dbg_y")
        xo_b = kb.buf("xo"); co_b = kb.buf("co")
        mod_phase(kb, c, cc, modw, modb, modrow_d, modrow_b, P)
        prep(kb, xs, ext, LH // 128, modrow_d, modrow_b, 0, hT_d, hT_b, 0, ident, identb32, P)
        prep(kb, ctx, ext, CTX // 128, modrow_d, modrow_b, 1, hT_d, hT_b, LH, ident, identb32, P)
        proj_fm(kb, hT_d, hT_b, (0, TT), w_in, 0, 1024, kT_d, kT_b, F32, P)
        proj_tm(kb, hT_d, hT_b, (0, TT), w_in, 1024, 1024, v_d, v_b, BF16, P)
        proj_fm(kb, hT_d, hT_b, (0, TT), w_in, 2048, 4096, qT_d, qT_b, F32, P)
        proj_fm(kb, hT_d, hT_b, (0, TT), w_in, 6144, 4096, zT_d, zT_b, BF16, P, func=AF.Silu)
        wa_mixer(kb, LT, kT_d, kT_b, qT_d, qT_b, zT_d, zT_b, v_d, v_b, ogT_d, ogT_b, cos, sin, perm, mask, sink, identb, P, PB)
        outproj(kb, ogT_d, ogT_b, D, LT + CTX, w_out, y_d, y_b, P)
        epilogue(kb, xs[128:128 + LT, :], ext, [(y_d, y_b)], LT // 128, 0, modrow_d, modrow_b, 0, lng, lnb, xo, xo_b)
        epilogue(kb, ctx, ext, [(y_d, y_b)], CTX // 128, LT, modrow_d, modrow_b, 1, lng, lnb, co, co_b)
        kb.finish()
    return nc


def rt_consts(ncores, LT, core):
    C = 128
    i = np.arange(C)
    j = np.arange(C)[:, None]
    ii = np.arange(C)[None, :]
    E1 = np.where(ii >= j, ii - j, 0).astype(np.float32)
    E2 = np.where(j > ii, j - ii, 0).astype(np.float32)
    R1 = np.broadcast_to((i + 1).astype(np.float32), (128, C)).copy()
    R2 = np.broadcast_to((C - i).astype(np.float32), (128, C)).copy()
    col = np.zeros((128, 4), np.float32)
    col[:, 0] = C - 1 - np.arange(C)
    col[:, 1] = np.arange(C)
    col[:, 2] = C
    col[:, 3] = C
    BIG = 1e12
    ecf = np.full((ncores + 1,), BIG, np.float32)
    ecb = np.full((ncores + 1,), BIG, np.float32)
    for c2 in range(ncores):
        if c2 < core:
            ecf[c2] = LT * (core - 1 - c2)
        if c2 > core:
            ecb[c2] = LT * (c2 - core - 1)
    ecf[ncores] = LT * core
    ecb[ncores] = LT * (ncores - 1 - core)
    ec = np.broadcast_to(np.stack([ecf, ecb])[None], (128, 2, ncores + 1)).copy().astype(np.float32)
    pos = core * LT + np.arange(LT)
    invf = 10000.0 ** (-np.arange(128, dtype=np.float32) / 128)
    ang = pos[None, :].astype(np.float32) * invf[:, None]
    return dict(E1=E1, E2=E2, R1=R1, R2=R2, dcol=col, ec=ec, rcos=np.cos(ang).astype(np.float32), rsin=np.sin(ang).astype(np.float32),
                ident32=np.eye(128, dtype=np.float32), identb=np.eye(128).astype(ml_dtypes.bfloat16))


def rt_mixer(kb, ncores, LT, kT_d, kT_b, qT_d, qT_b, v_d, v_b, z_d, z_b, ogT_d, ogT_b, lg_d, cst, aall_d, aout_d, aout_b, identb_d, P, PB, with_out=True):
    C = 128
    NTL = LT // C
    T = LT + CTX
    NT = T // C
    kscale = 256 ** -0.5
    ext = kb.buf("rt_ext")
    with phase(kb):
        identb, identbb = kb.sb([128, 128], BF16)
        kb.dma("sp", identb[:], identb_d[:, :], writes=[identbb])
        lg, lgb = kb.sb([128, 32], F32)
        kb.dma("sp", lg[:], lg_d.rearrange("a b -> (a b)").partition_broadcast(128), writes=[lgb])
        cb = kb.buf("consts")
        E1, _ = kb.sb([128, 128], F32); E2, _ = kb.sb([128, 128], F32); R1, _ = kb.sb([128, 128], F32); R2, _ = kb.sb([128, 128], F32)
        dcol, _ = kb.sb([128, 4], F32); ec, _ = kb.sb([128, 2, ncores + 1], F32)
        rcos, _ = kb.sb([128, LT], F32); rsin, _ = kb.sb([128, LT], F32)
        for i, (t, nm) in enumerate([(E1, "E1"), (E2, "E2"), (R1, "R1"), (R2, "R2"), (dcol, "dcol"), (rcos, "rcos"), (rsin, "rsin")]):
            kb.dma("sp", t[:], cst[nm][:, :], writes=[cb.sub(i)])
        kb.dma("sp", ec[:], cst["ec"][:, :, :], writes=[cb.sub(99)])
        DmT, DmTb = kb.sb([128, 16, 128], F32)
        dq, dqb_ = kb.sb([128, 2, 16, 128], F32)
        kd, kdb = kb.sb([128, 2, 16], F32)
        gC, gCb = kb.sb([128, 2, 16], F32)
        coef, coefb = kb.sb([128, 2, 16, ncores + 1], F32)
        tmp, tmpb = kb.sb([128, 128], F32)
        for h in range(16):
            lf = lg[:, h:h + 1]; lb_ = lg[:, 16 + h:17 + h]
            kb.op("dve", lambda e: e.tensor_scalar(out=tmp[:], in0=E1[:], scalar1=lf, scalar2=None, op0=ALU.mult), reads=[cb, lgb], writes=[tmpb])
            kb.op("dve", lambda e: e.scalar_tensor_tensor(out=tmp[:], in0=E2[:], scalar=lb_, in1=tmp[:], op0=ALU.mult, op1=ALU.add), reads=[cb, lgb, tmpb], writes=[tmpb])
            kb.op("act", lambda e: e.activation(out=DmT[:, h, :], in_=tmp[:], func=AF.Exp), reads=[tmpb], writes=[DmTb.sub(h)])
            kb.op("act", lambda e: e.activation(out=dq[:, 0, h, :], in_=R1[:], func=AF.Exp, scale=lf), reads=[cb, lgb], writes=[dqb_.sub(h)])
            kb.op("act", lambda e: e.activation(out=dq[:, 1, h, :], in_=R2[:], func=AF.Exp, scale=lb_), reads=[cb, lgb], writes=[dqb_.sub(16 + h)])
            kb.op("act", lambda e: e.activation(out=kd[:, 0, h:h + 1], in_=dcol[:, 0:1], func=AF.Exp, scale=lf), reads=[cb, lgb], writes=[kdb.sub(h)])
            kb.op("act", lambda e: e.activation(out=kd[:, 1, h:h + 1], in_=dcol[:, 1:2], func=AF.Exp, scale=lb_), reads=[cb, lgb], writes=[kdb.sub(16 + h)])
            kb.op("act", lambda e: e.activation(out=gC[:, 0, h:h + 1], in_=dcol[:, 2:3], func=AF.Exp, scale=lf), reads=[cb, lgb], writes=[gCb.sub(h)])
            kb.op("act", lambda e: e.activation(out=gC[:, 1, h:h + 1], in_=dcol[:, 3:4], func=AF.Exp, scale=lb_), reads=[cb, lgb], writes=[gCb.sub(16 + h)])
            kb.op("act", lambda e: e.activation(out=coef[:, 0, h, :], in_=ec[:, 0, :], func=AF.Exp, scale=lf), reads=[cb, lgb], writes=[coefb.sub(h)])
            kb.op("act", lambda e: e.activation(out=coef[:, 1, h, :], in_=ec[:, 1, :], func=AF.Exp, scale=lb_), reads=[cb, lgb], writes=[coefb.sub(16 + h)])
        KT, KTb = kb.sb([128, 2, T], BF16)
        QT, QTb = kb.sb([128, 2, T], BF16)
        Ktok, Ktokb = kb.sb([128, NT, 256], BF16)
        SbAll, SbAllb = kb.sb([128, NT, 2, 512], BF16)
        OGT, OGTb = kb.sb([128, 4, T], BF16)
        S32 = [kb.sb([128, 2, 512], F32) for _ in range(2)]
        Sfb, Sfbb = kb.sb([128, 2, 512], BF16)
        RAW = Rot([kb.sb([128, 2, 512], F32) for _ in range(2)])
        T1 = Rot([kb.sb([128, 512], F32) for _ in range(2)])
        T2 = Rot([kb.sb([128, 512], F32) for _ in range(2)])
        VV = Rot([kb.sb([128, 512], BF16) for _ in range(3)])
        ZZ = Rot([kb.sb([128, 512], BF16) for _ in range(2)])
        KD = Rot([kb.sb([128, 256], BF16) for _ in range(2)])
        AT = Rot([kb.sb([128, 128], BF16) for _ in range(2)])
        QD = Rot([kb.sb([128, 2, 2, 128], BF16) for _ in range(2)])
        OO = Rot([kb.sb([128, 512], F32) for _ in range(2)])
        OB = Rot([kb.sb([128, 512], BF16) for _ in range(2)])
        ST = Rot([kb.sb([128, 12], F32) for _ in range(2)])
        AIN = Rot([kb.sb([128, 2, 512], F32) for _ in range(2)])

        def rope_pair(raw_d, raw_b, h, dst, dstb, scale_lat, scale_ctx):
            for b0 in range(0, LT, 512):
                m = min(512, LT - b0)
                rw, rwb = RAW.next()
                kb.dma("sp", rw[:, :, 0:m], raw_d[h * 256:(h + 1) * 256, b0:b0 + m].rearrange("(a p) t -> p a t", p=128), reads=[raw_b], writes=[rwb])
                for a in range(2):
                    o = 1 - a
                    t1, t1b = T1.next(); t2, t2b = T2.next()
                    kb.op("dve", lambda e: e.tensor_tensor(out=t1[:, 0:m], in0=rw[:, a, 0:m], in1=rcos[:, b0:b0 + m], op=ALU.mult), reads=[rwb, cb], writes=[t1b])
                    kb.op("pool", lambda e: e.tensor_tensor(out=t2[:, 0:m], in0=rw[:, o, 0:m], in1=rsin[:, b0:b0 + m], op=ALU.mult), reads=[rwb, cb], writes=[t2b])
                    kb.op("dve", lambda e: e.tensor_tensor(out=t1[:, 0:m], in0=t1[:, 0:m], in1=t2[:, 0:m], op=(ALU.subtract if a == 0 else ALU.add)), reads=[t1b, t2b], writes=[t1b])
                    kb.op("act", lambda e: e.activation(out=dst[:, a, b0:b0 + m], in_=t1[:, 0:m], func=AF.Copy, scale=scale_lat), reads=[t1b], writes=[dstb.sub((a, b0))])
            rw, rwb = RAW.next()
            kb.dma("sp", rw[:, :, 0:CTX], raw_d[h * 256:(h + 1) * 256, LT:T].rearrange("(a p) t -> p a t", p=128), reads=[raw_b], writes=[rwb])
            kb.op("act", lambda e: e.activation(out=dst[:, :, LT:T], in_=rw[:, :, 0:CTX], func=AF.Copy, scale=scale_ctx), reads=[rwb], writes=[dstb.sub("ctx")])

        def load_v(n, h):
            v, vb = VV.next()
            kb.dma("sp", v[:], v_d[n * C:(n + 1) * C, h * 512:(h + 1) * 512], reads=[v_b], writes=[vb])
            return v, vb

        def state_update(d, h, n, v, vb):
            S, Sb_ = S32[d]
            kdt, kdtb = KD.next()
            kb.op("pool", lambda e: e.tensor_scalar(out=kdt[:], in0=Ktok[:, n, :], scalar1=kd[:, d, h:h + 1], scalar2=None, op0=ALU.mult), reads=[Ktokb, kdb], writes=[kdtb])
            for a in range(2):
                pt, pb = P.next()
                kb.op("pe", lambda e: e.matmul(pt[:, 0:512], lhsT=kdt[:, a * 128:(a + 1) * 128], rhs=v[:], start=True, stop=True), reads=[kdtb, vb], writes=[pb])
                kb.op("dve", lambda e: e.scalar_tensor_tensor(out=S[:, a, :], in0=S[:, a, :], scalar=gC[:, d, h:h + 1], in1=pt[:, 0:512], op0=ALU.mult, op1=ALU.add),
                      reads=[Sb_, gCb, pb], writes=[Sb_])

        def seq(h, chunks, emit):
            Sb32, Sb32b = S32[1]
            for n in reversed(chunks):
                kb.op("act", lambda e: e.copy(out=SbAll[:, n, :, :], in_=Sb32[:]), reads=[Sb32b], writes=[SbAllb.sub(n)])
                v, vb = load_v(n, h)
                state_update(1, h, n, v, vb)
            Sf32, Sf32b = S32[0]
            for n in chunks:
                cs = slice(n * C, (n + 1) * C)
                v, vb = load_v(n, h)
                if emit:
                    kb.op("act", lambda e: e.copy(out=Sfb[:], in_=Sf32[:]), reads=[Sf32b], writes=[Sfbb])
                    pa, pab = P.next()
                    for a in range(2):
                        kb.op("pe", lambda e, a=a: e.matmul(pa[:, 0:128], lhsT=KT[:, a, cs], rhs=QT[:, a, cs], start=(a == 0), stop=(a == 1)), reads=[KTb, QTb], writes=[pab])
                    at, atb = AT.next()
                    kb.op("dve", lambda e: e.tensor_tensor(out=at[:], in0=pa[:, 0:128], in1=DmT[:, h, :], op=ALU.mult), reads=[pab, DmTb], writes=[atb])
                    qd, qdb2 = QD.next()
                    for d in range(2):
                        kb.op("pool", lambda e, d=d: e.tensor_tensor(out=qd[:, d, :, :], in0=QT[:, :, cs], in1=dq[:, d, h:h + 1, :].to_broadcast([128, 2, 128]), op=ALU.mult),
                              reads=[QTb, dqb_], writes=[qdb2.sub(d)])
                    po, pob = P.next()
                    kb.op("pe", lambda e: e.matmul(po[:, 0:512], lhsT=at[:], rhs=v[:], start=True, stop=False), reads=[atb, vb], writes=[pob])
                    for a in range(2):
                        kb.op("pe", lambda e, a=a: e.matmul(po[:, 0:512], lhsT=qd[:, 0, a, :], rhs=Sfb[:, a, :], start=False, stop=False), reads=[qdb2, Sfbb], writes=[pob])
                    for a in range(2):
                        kb.op("pe", lambda e, a=a: e.matmul(po[:, 0:512], lhsT=qd[:, 1, a, :], rhs=SbAll[:, n, a, :], start=False, stop=(a == 1)), reads=[qdb2, SbAllb], writes=[pob])
                    o, ob = OO.next(); st, stb = ST.next()
                    kb.op("act", lambda e: e.copy(out=o[:], in_=po[:, 0:512]), reads=[pob], writes=[ob])
                    kb.op("dve", lambda e: e.bn_stats(out=st[:, 0:6], in_=o[:]), reads=[ob], writes=[stb])
                    kb.op("dve", lambda e: e.bn_aggr(out=st[:, 6:8], in_=st[:, 0:6]), reads=[stb], writes=[stb])
                    kb.op("dve", lambda e: e.tensor_scalar(out=st[:, 8:9], in0=st[:, 7:8], scalar1=LN_EPS, scalar2=None, op0=ALU.add), reads=[stb], writes=[stb])
                    kb.op("act", lambda e: e.activation(out=st[:, 8:9], in_=st[:, 8:9], func=AF.Sqrt), reads=[stb], writes=[stb])
                    kb.op("dve", lambda e: e.reciprocal(out=st[:, 9:10], in_=st[:, 8:9]), reads=[stb], writes=[stb])
                    kb.op("dve", lambda e: e.tensor_scalar(out=o[:], in0=o[:], scalar1=st[:, 6:7], scalar2=st[:, 9:10], op0=ALU.subtract, op1=ALU.mult), reads=[ob, stb], writes=[ob])
                    z, zb = ZZ.next()
                    kb.dma("sp", z[:], z_d[n * C:(n + 1) * C, h * 512:(h + 1) * 512], reads=[z_b], writes=[zb])
                    obf, obfb = OB.next()
                    kb.op("pool", lambda e: e.tensor_tensor(out=obf[:], in0=o[:], in1=z[:], op=ALU.mult), reads=[ob, zb], writes=[obfb])
                    ptb, ptbb = PB.next()
                    for a in range(4):
                        kb.op("pe", lambda e, a=a: e.transpose(out=ptb[:, a * 128:(a + 1) * 128], in_=obf[:, a * 128:(a + 1) * 128], identity=identb[:]), reads=[obfb, identbb], writes=[ptbb])
                    kb.op("act", lambda e: e.copy(out=OGT[:, :, cs], in_=ptb[:, 0:512].rearrange("p (a b) -> p a b", b=128)), reads=[ptbb], writes=[OGTb.sub(n)])
                state_update(0, h, n, v, vb)

        for h in range(16):
            rope_pair(kT_d, kT_b, h, KT, KTb, kscale, kscale)
            if with_out:
                rope_pair(qT_d, qT_b, h, QT, QTb, 1.0, 1.0)
            for n in range(NT):
                ptb, ptbb = PB.next()
                for a in range(2):
                    kb.op("pe", lambda e, a=a: e.transpose(out=ptb[:, a * 128:(a + 1) * 128], in_=KT[:, a, n * C:(n + 1) * C], identity=identb[:]), reads=[KTb, identbb], writes=[ptbb])
                kb.op("act", lambda e: e.copy(out=Ktok[:, n, :], in_=ptb[:, 0:256]), reads=[ptbb], writes=[Ktokb.sub(n)])
            for d in range(2):
                kb.op("pool", lambda e, d=d: e.memset(S32[d][0][:], 0.0), writes=[S32[d][1]])
            seq(h, [NTL, NTL + 1], with_out)
            for d in range(2):
                kb.dma("sp", aout_d[2 + d, h].rearrange("(a p) v -> p a v", p=128), S32[d][0][:], reads=[S32[d][1]], writes=[aout_b.sub((2 + d, h))])
            for d in range(2):
                S, Sb_ = S32[d]
                if not with_out:
                    kb.op("pool", lambda e: e.memset(S[:], 0.0), writes=[Sb_])
                    continue
                for c2 in range(ncores + 1):
                    ai, aib = AIN.next()
                    kb.dma("sp", ai[:], aall_d[c2, d, h].rearrange("(a p) v -> p a v", p=128), reads=[ext], writes=[aib])
                    if c2 == 0:
                        kb.op("dve", lambda e: e.tensor_scalar(out=S[:], in0=ai[:], scalar1=coef[:, d, h, c2:c2 + 1], scalar2=None, op0=ALU.mult), reads=[aib, coefb], writes=[Sb_])
                    else:
                        kb.op("dve", lambda e: e.scalar_tensor_tensor(out=S[:], in0=ai[:], scalar=coef[:, d, h, c2:c2 + 1], in1=S[:], op0=ALU.mult, op1=ALU.add), reads=[aib, coefb, Sb_], writes=[Sb_])
            seq(h, list(range(NTL)), with_out)
            for d in range(2):
                kb.dma("sp", aout_d[d, h].rearrange("(a p) v -> p a v", p=128), S32[d][0][:], reads=[S32[d][1]], writes=[aout_b.sub((d, h))])
            if with_out:
                kb.dma("sp", ogT_d[h * 512:(h + 1) * 512, :].rearrange("(a p) t -> p a t", p=128), OGT[:], reads=[OGTb], writes=[ogT_b.sub(h)])


def build_layer1(ncores, LT, states_only=False):
    T = LT + CTX
    full = not states_only
    nc = bass.Bass("TRN2", target_bir_lowering=False)

    def inp(name, shape, dt=F32):
        return nc.dram_tensor(name, list(shape), dt, kind="ExternalInput").ap()
    xs = inp("xs", [LT, D]); ctx = inp("ctx", [CTX, D]); c = inp("c", [D]); cc = inp("cc", [D])
    modw = inp("modw", [D, 3 * D]); modb = inp("modb", [3 * D])
    lng = inp("lng", [D]) if full else None; lnb = inp("lnb", [D]) if full else None
    w_in = inp("w_in", [D, 24576 if full else 12288]); lg = inp("lg", [2, 16])
    w_out = inp("w_out", [8192, D]) if full else None
    cst = dict(E1=inp("E1", [128, 128]), E2=inp("E2", [128, 128]), R1=inp("R1", [128, 128]), R2=inp("R2", [128, 128]),
               dcol=inp("dcol", [128, 4]), ec=inp("ec", [128, 2, ncores + 1]), rcos=inp("rcos", [128, LT]), rsin=inp("rsin", [128, LT]))
    ident32 = inp("ident32", [128, 128]); identb = inp("identb", [128, 128], BF16)
    aall = inp("aall", [ncores + 1, 2, 16, 256, 512]) if full else None
    xo = nc.dram_tensor("xo", [LT, D], F32, kind="ExternalOutput").ap() if full else None
    co = nc.dram_tensor("co", [CTX, D], F32, kind="ExternalOutput").ap() if full else None
    aout = nc.dram_tensor("aout", [4, 16, 256, 512], F32, kind="ExternalOutput").ap()
    with contextlib.ExitStack() as es:
        kb = KB(nc, es)
        P = Rot([kb.ps([128, 512], F32) for _ in range(6)])
        PB = Rot([kb.ps([128, 1024], BF16) for _ in range(2)])
        ident, identb32 = kb.sb([128, 128], F32)
        kb.dma("sp", ident[:], ident32[:, :], writes=[identb32])
        ext = kb.buf("ext")
        modrow_d, modrow_b = kb.dram([2, 3 * D], F32)
        hT_d, hT_b = kb.dram([D, T], BF16)
        kT_d, kT_b = kb.dram([D, T], F32)
        qT_d, qT_b = kb.dram([D, T], F32)
        v_d, v_b = kb.dram([T, 8192], BF16)
        z_d, z_b = kb.dram([T, 8192], BF16)
        ogT_d, ogT_b = kb.dram([8192, T], BF16)
        ya_d, ya_b = kb.dram([T, D], F32)
        yb_d, yb_b = kb.dram([T, D], F32)
        xo_b = kb.buf("xo"); co_b = kb.buf("co"); aout_b = kb.buf("aout")
        mod_phase(kb, c, cc, modw, modb, modrow_d, modrow_b, P)
        prep(kb, xs, ext, LT // 128, modrow_d, modrow_b, 0, hT_d, hT_b, 0, ident, identb32, P)
        prep(kb, ctx, ext, CTX // 128, modrow_d, modrow_b, 1, hT_d, hT_b, LT, ident, identb32, P)
        proj_fm(kb, hT_d, hT_b, (0, T), w_in, 0, 4096, kT_d, kT_b, F32, P)
        proj_tm(kb, hT_d, hT_b, (0, T), w_in, 4096, 8192, v_d, v_b, BF16, P)
        if full:
            proj_fm(kb, hT_d, hT_b, (0, T), w_in, 12288, 4096, qT_d, qT_b, F32, P)
            proj_tm(kb, hT_d, hT_b, (0, T), w_in, 16384, 8192, z_d, z_b, BF16, P, func=AF.Silu)
        rt_mixer(kb, ncores, LT, kT_d, kT_b, qT_d, qT_b, v_d, v_b, z_d, z_b, ogT_d, ogT_b, lg, cst, aall, aout, aout_b, identb, P, PB, with_out=full)
        if full:
            outproj(kb, ogT_d, ogT_b, D, T, w_out, ya_d, ya_b, P, k0=0)
            outproj(kb, ogT_d, ogT_b, D, T, w_out, yb_d, yb_b, P, k0=D)
            epilogue(kb, xs, ext, [(ya_d, ya_b), (yb_d, yb_b)], LT // 128, 0, modrow_d, modrow_b, 0, lng, lnb, xo, xo_b)
            epilogue(kb, ctx, ext, [(ya_d, ya_b), (yb_d, yb_b)], CTX // 128, LT, modrow_d, modrow_b, 1, lng, lnb, co, co_b)
        kb.finish()
    return nc


def run_layer1(ncores, LT, x_in, ctx_in, c, c_ctx, mod_w, mod_b, ln_g, ln_b, w_in, w_out, lg):
    base = []
    for ci in range(ncores):
        m = dict(xs=np.ascontiguousarray(x_in[ci * LT:(ci + 1) * LT]), ctx=ctx_in, c=c, cc=c_ctx, modw=mod_w, modb=mod_b, lg=lg)
        m.update(rt_consts(ncores, LT, ci))
        base.append(m)
    w_kv = np.ascontiguousarray(w_in[:, :12288])
    nc1 = build_layer1(ncores, LT, states_only=True)
    res = run_bass_kernel_spmd(nc1, [dict(m, w_in=w_kv) for m in base], core_ids=list(range(ncores)))
    aall = np.stack([r["aout"][0:2] for r in res.results] + [res.results[0]["aout"][2:4]], 0)
    del res
    nc = build_layer1(ncores, LT)
    res = run_bass_kernel_spmd(nc, [dict(m, w_in=w_in, w_out=w_out, aall=aall, lng=ln_g, lnb=ln_b) for m in base], core_ids=list(range(ncores)))
    xo = np.concatenate([r["xo"] for r in res.results], 0)
    return xo, res.results[0]["co"]


NA_HALO = 384


def na_consts(ncores, LT, core):
    NTL = LT // 128
    R = ncores * LT // 64
    NEG = -1e30
    M = np.full((NTL, 128, 14, 64), NEG, np.float32)
    qc = np.arange(64)
    cs = np.clip(qc - 8, 0, 48)
    kc = np.arange(64)
    colok = (kc[None, :] >= cs[:, None]) & (kc[None, :] < cs[:, None] + 16)
    for j in range(NTL):
        J = core * NTL + j
        for rl in range(2):
            r = 2 * J + rl
            rs = min(max(r - 4, 0), R - 8)
            for kr in range(14):
                krow = 2 * (J - 3) + kr
                if rs <= krow < rs + 8:
                    M[j, rl * 64:(rl + 1) * 64, kr, :] = np.where(colok, 0.0, NEG)
    return dict(namask=M.reshape(NTL, 128, 896), ident32=np.eye(128, dtype=np.float32), identb=np.eye(128).astype(ml_dtypes.bfloat16))


def na_bias_layout(rpb):
    rl = np.arange(2)[:, None, None, None]; qc = np.arange(64)[None, :, None, None]
    kr = np.arange(14)[None, None, :, None]; kc = np.arange(64)[None, None, None, :]
    rho = np.broadcast_to(kr - rl + 1, (2, 64, 14, 64))
    col = np.clip(np.broadcast_to(kc - qc + 15, (2, 64, 14, 64)), 0, 30)
    g = rpb[:, rho, col]
    return np.ascontiguousarray(g.reshape(32, 128, 896).astype(np.float32))


def na_mixer(kb, LT, kT_d, kT_b, qT_d, qT_b, zT_d, zT_b, v_d, v_b, ogT_d, ogT_b, rpb_d, mask_d, identb_d, P, PB):
    NTL = LT // 128
    H = NA_HALO
    LH = LT + 2 * H
    TT = LH + CTX
    scale = 128 ** -0.5
    with phase(kb):
        identb, identbb = kb.sb([128, 128], BF16)
        kb.dma("sp", identb[:], identb_d[:, :], writes=[identbb])
        mask, maskb = kb.sb([128, NTL, 896], F32)
        kb.dma("sp", mask[:], mask_d.rearrange("j p n -> p j n"), writes=[maskb])
        G = Rot([kb.sb([128, 14, 64], F32) for _ in range(2)])
        A = Attn(kb, P, PB, identb, identbb, nk_max=1152)
        K, Kb = kb.sb([128, TT], BF16)
        V, Vb = kb.sb([128, TT // 128, 128], BF16)
        Q, Qb = kb.sb([128, LT + CTX], BF16)
        Z, Zb = kb.sb([128, LT + CTX], BF16)
        OG, OGb = kb.sb([128, LT + CTX], BF16)
        for h in range(32):
            g7, g7b = G.next()
            kb.dma("sp", g7[:], rpb_d[h].rearrange("p (a b) -> p a b", b=64), writes=[g7b])
            g7f = g7[:].rearrange("p a b -> p (a b)")
            kb.dma("sp", K[:], kT_d[h * 128:(h + 1) * 128, :], reads=[kT_b], writes=[Kb])
            kb.dma("sp", V[:], v_d[:, h * 128:(h + 1) * 128].rearrange("(n p) d -> p n d", p=128), reads=[v_b], writes=[Vb])
            kb.dma("sp", Q[:, 0:LT], qT_d[h * 128:(h + 1) * 128, H:H + LT], reads=[qT_b], writes=[Qb.sub(0)])
            kb.dma("sp", Q[:, LT:LT + CTX], qT_d[h * 128:(h + 1) * 128, LH:TT], reads=[qT_b], writes=[Qb.sub(1)])
            kb.dma("sp", Z[:, 0:LT], zT_d[h * 128:(h + 1) * 128, H:H + LT], reads=[zT_b], writes=[Zb.sub(0)])
            kb.dma("sp", Z[:, LT:LT + CTX], zT_d[h * 128:(h + 1) * 128, LH:TT], reads=[zT_b], writes=[Zb.sub(1)])
            for j in range(NTL):
                k0 = j * 128
                segs = [(K[:, k0:k0 + 448], Kb, g7f[:, 0:448], g7b, mask[:, j, 0:448], maskb),
                        (K[:, k0 + 448:k0 + 896], Kb, g7f[:, 448:896], g7b, mask[:, j, 448:896], maskb),
                        (K[:, LH:TT], Kb, None, None)]
                vt = [(V[:, j + i, :], Vb) for i in range(7)] + [(V[:, LH // 128 + i, :], Vb) for i in range(2)]
                A.run(Q[:, k0:k0 + 128], Qb, segs, vt, None, None, Z[:, k0:k0 + 128], Zb, OG[:, k0:k0 + 128], OGb.sub(j), scale)
            for j in range(2):
                segs = [(K[:, LH:TT], Kb, None, None)]
                vt = [(V[:, LH // 128 + i, :], Vb) for i in range(2)]
                A.run(Q[:, LT + j * 128:LT + (j + 1) * 128], Qb, segs, vt, None, None, Z[:, LT + j * 128:LT + (j + 1) * 128], Zb,
                      OG[:, LT + j * 128:LT + (j + 1) * 128], OGb.sub(NTL + j), scale)
            kb.dma("sp", ogT_d[h * 128:(h + 1) * 128, :], OG[:], reads=[OGb], writes=[ogT_b.sub(h)])


def build_layer2(LT):
    H = NA_HALO
    LH = LT + 2 * H
    TT = LH + CTX
    nc = bass.Bass("TRN2", target_bir_lowering=False)

    def inp(name, shape, dt=F32):
        return nc.dram_tensor(name, list(shape), dt, kind="ExternalInput").ap()
    xs = inp("xs", [LH, D]); ctx = inp("ctx", [CTX, D]); c = inp("c", [D]); cc = inp("cc", [D])
    modw = inp("modw", [D, 3 * D]); modb = inp("modb", [3 * D]); lng = inp("lng", [D]); lnb = inp("lnb", [D])
    w_in = inp("w_in", [D, 16384]); w_out = inp("w_out", [D, D]); rpb = inp("rpb", [32, 128, 896])
    namask = inp("namask", [LT // 128, 128, 896]); ident32 = inp("ident32", [128, 128]); identb = inp("identb", [128, 128], BF16)
    xo = nc.dram_tensor("xo", [LT, D], F32, kind="ExternalOutput").ap()
    co = nc.dram_tensor("co", [CTX, D], F32, kind="ExternalOutput").ap()
    with contextlib.ExitStack() as es:
        kb = KB(nc, es)
        P = Rot([kb.ps([128, 512], F32) for _ in range(6)])
        PB = Rot([kb.ps([128, 1024], BF16) for _ in range(2)])
        ident, identb32 = kb.sb([128, 128], F32)
        kb.dma("sp", ident[:], ident32[:, :], writes=[identb32])
        ext = kb.buf("ext")
        modrow_d, modrow_b = kb.dram([2, 3 * D], F32)
        hT_d, hT_b = kb.dram([D, TT], BF16)
        kT_d, kT_b = kb.dram([D, TT], BF16)
        qT_d, qT_b = kb.dram([D, TT], BF16)
        zT_d, zT_b = kb.dram([D, TT], BF16)
        v_d, v_b = kb.dram([TT, D], BF16)
        ogT_d, ogT_b = kb.dram([D, LT + CTX], BF16)
        y_d, y_b = kb.dram([LT + CTX, D], F32)
        xo_b = kb.buf("xo"); co_b = kb.buf("co")
        mod_phase(kb, c, cc, modw, modb, modrow_d, modrow_b, P)
        prep(kb, xs, ext, LH // 128, modrow_d, modrow_b, 0, hT_d, hT_b, 0, ident, identb32, P)
        prep(kb, ctx, ext, CTX // 128, modrow_d, modrow_b, 1, hT_d, hT_b, LH, ident, identb32, P)
        proj_fm(kb, hT_d, hT_b, (0, TT), w_in, 0, 4096, kT_d, kT_b, BF16, P)
        proj_tm(kb, hT_d, hT_b, (0, TT), w_in, 4096, 4096, v_d, v_b, BF16, P)
        proj_fm(kb, hT_d, hT_b, (0, TT), w_in, 8192, 4096, qT_d, qT_b, BF16, P)
        proj_fm(kb, hT_d, hT_b, (0, TT), w_in, 12288, 4096, zT_d, zT_b, BF16, P, func=AF.Silu)
        na_mixer(kb, LT, kT_d, kT_b, qT_d, qT_b, zT_d, zT_b, v_d, v_b, ogT_d, ogT_b, rpb, namask, identb, P, PB)
        outproj(kb, ogT_d, ogT_b, D, LT + CTX, w_out, y_d, y_b, P)
        epilogue(kb, xs[H:H + LT, :], ext, [(y_d, y_b)], LT // 128, 0, modrow_d, modrow_b, 0, lng, lnb, xo, xo_b)
        epilogue(kb, ctx, ext, [(y_d, y_b)], CTX // 128, LT, modrow_d, modrow_b, 1, lng, lnb, co, co_b)
        kb.finish()
    return nc


def halo_slabs(x, ncores, LT, H):
    L = ncores * LT
    out = []
    for ci in range(ncores):
        xs = np.zeros((LT + 2 * H, x.shape[1]), np.float32)
        lo, hi = ci * LT - H, ci * LT + LT + H
        a, b = max(lo, 0), min(hi, L)
        xs[a - lo:b - lo] = x[a:b]
        out.append(xs)
    return out


def run_layer2(ncores, LT, x_in, ctx_in, c, c_ctx, mod_w, mod_b, ln_g, ln_b, w_in, w_out, rpb):
    nc = build_layer2(LT)
    rpb = na_bias_layout(np.asarray(rpb, np.float32))
    slabs = halo_slabs(x_in, ncores, LT, NA_HALO)
    in_maps = []
    for ci in range(ncores):
        m = dict(xs=slabs[ci], ctx=ctx_in, c=c, cc=c_ctx, modw=mod_w, modb=mod_b, lng=ln_g, lnb=ln_b, w_in=w_in, w_out=w_out, rpb=rpb)
        m.update(na_consts(ncores, LT, ci))
        in_maps.append(m)
    res = run_bass_kernel_spmd(nc, in_maps, core_ids=list(range(ncores)))
    return np.concatenate([r["xo"] for r in res.results], 0), res.results[0]["co"]


def s5_consts(ncores, LT, core):
    fl = np.zeros((128, 2, ncores), np.float32)
    for c2 in range(ncores):
        fl[:, 0, c2] = 1.0 if c2 < core else 0.0
        fl[:, 1, c2] = 1.0 if c2 > core else 0.0
    J = np.eye(128, dtype=np.float32)[::-1].copy()
    return dict(flags=fl, J32=J, ident32=np.eye(128, dtype=np.float32), identb=np.eye(128).astype(ml_dtypes.bfloat16))


def reverse_fm(kb, src_d, src_b, dst_d, dst_b, R, T, ident, identb, J, Jb, P):
    NT = T // 128
    with phase(kb):
        XI = Rot([kb.sb([128, 128], F32) for _ in range(3)])
        XT = Rot([kb.sb([128, 128], F32) for _ in range(3)])
        XO = Rot([kb.sb([128, 128], F32) for _ in range(3)])
        for rt in range(R // 128):
            for n in range(NT):
                xi, xib = XI.next(); xt, xtb = XT.next(); xo, xob = XO.next()
                kb.dma("sp", xi[:], src_d[rt * 128:(rt + 1) * 128, n * 128:(n + 1) * 128], reads=[src_b], writes=[xib])
                pt, pb = P.next()
                kb.op("pe", lambda e: e.transpose(out=pt[:, 0:128], in_=xi[:], identity=ident[:]), reads=[xib, identb], writes=[pb])
                kb.op("act", lambda e: e.copy(out=xt[:], in_=pt[:, 0:128]), reads=[pb], writes=[xtb])
                p2, p2b = P.next()
                kb.op("pe", lambda e: e.matmul(p2[:, 0:128], lhsT=xt[:], rhs=J[:], start=True, stop=True), reads=[xtb, Jb], writes=[p2b])
                kb.op("dve", lambda e: e.tensor_copy(out=xo[:], in_=p2[:, 0:128]), reads=[p2b], writes=[xob])
                m = NT - 1 - n
                kb.dma("sp", dst_d[rt * 128:(rt + 1) * 128, m * 128:(m + 1) * 128], xo[:], reads=[xob], writes=[dst_b.sub((rt, m))])


def s5_mixer(kb, ncores, LT, uT_d, uT_b, uR_d, uR_b, yf_d, yf_b, yr_d, yr_b, prm, xall_d, f0_d, flags_d, xout_d, xout_b, ident, identb32, identb_d, P, PB, emit_y=True):
    T = LT + CTX
    NE = max(LT, CTX)
    nsq = int(round(math.log2(LT)))
    assert 2 ** nsq == LT
    with phase(kb):
        identb, identbb = kb.sb([128, 128], BF16)
        kb.dma("sp", identb[:], identb_d[:, :], writes=[identbb])
        hp, hpb = kb.sb([128, 2], F32)
        kb.op("pool", lambda e: e.memset(hp[:, 0:1], math.pi / 2), writes=[hpb.sub(0)])
        kb.op("pool", lambda e: e.memset(hp[:, 1:2], 0.0), writes=[hpb.sub(1)])
        ones, onesb = kb.sb([128, NE], F32)
        kb.op("pool", lambda e: e.memset(ones[:], 1.0), writes=[onesb])
        TL = Rot([kb.sb([128, 128], F32) for _ in range(2)])

        def load_T(src2d, dstt, dstb):
            tl, tlb = TL.next()
            kb.dma("sp", tl[:], src2d, writes=[tlb])
            pt, pb = P.next()
            kb.op("pe", lambda e: e.transpose(out=pt[:, 0:128], in_=tl[:], identity=ident[:]), reads=[tlb, identb32], writes=[pb])
            kb.op("act", lambda e: e.copy(out=dstt, in_=pt[:, 0:128]), reads=[pb], writes=[dstb])

        def T128(name):
            return kb.sb([128, 128], F32, name=name)

        def tt(out, ob, a, ab, b, bb, op):
            kb.op("dve", lambda e: e.tensor_tensor(out=out, in0=a, in1=b, op=op), reads=[ab, bb], writes=[ob])

        PR = []
        tA, tAb = T128("s5_tA"); tB, tBb = T128("s5_tB"); tC, tCb = T128("s5_tC")
        w1, w1b = kb.sb([128, 128, 16], F32, name="s5_w1"); w2, w2b = kb.sb([128, 128, 16], F32, name="s5_w2")
        Bre, Breb = kb.sb([128, 128, 16], F32, name="s5_Bre"); Bim, Bimb = kb.sb([128, 128, 16], F32, name="s5_Bim")
        kb.dma("sp", Bre[:], prm["b_re"].rearrange("g s n -> (g s) n").rearrange("(t p) n -> p t n", p=128), writes=[Breb])
        kb.dma("sp", Bim[:], prm["b_im"].rearrange("g s n -> (g s) n").rearrange("(t p) n -> p t n", p=128), writes=[Bimb])
        for d in range(2):
            are, areb = T128(f"are{d}"); aim, aimb = T128(f"aim{d}"); dtv, dtvb = T128(f"dt{d}")
            load_T(prm["a_re"][d].rearrange("(t a) s -> t (a s)", a=2), are[:], areb)
            load_T(prm["a_im"][d].rearrange("(t a) s -> t (a s)", a=2), aim[:], aimb)
            load_T(prm["ldt_exp"][d].rearrange("(t p) -> t p", p=128), dtv[:], dtvb)
            kb.op("act", lambda e: e.activation(out=dtv[:], in_=dtv[:], func=AF.Exp), reads=[dtvb], writes=[dtvb])
            rr, rrb = T128(f"r{d}"); cc_, ccb = T128(f"c{d}"); ss, ssb = T128(f"s{d}")
            tt(tA[:], tAb, are[:], areb, dtv[:], dtvb, ALU.mult)
            kb.op("act", lambda e: e.activation(out=rr[:], in_=tA[:], func=AF.Exp), reads=[tAb], writes=[rrb])
            tt(tB[:], tBb, aim[:], aimb, dtv[:], dtvb, ALU.mult)
            kb.op("act", lambda e: e.activation(out=cc_[:], in_=tB[:], func=AF.Sin, bias=hp[:, 0:1], scale=1.0 / 16), reads=[tBb, hpb], writes=[ccb])
            kb.op("act", lambda e: e.activation(out=ss[:], in_=tB[:], func=AF.Sin, bias=hp[:, 1:2], scale=1.0 / 16), reads=[tBb, hpb], writes=[ssb])

            def cdouble(c_, cb_, s_, sb_):
                tt(tA[:], tAb, c_, cb_, c_, cb_, ALU.mult)
                tt(tB[:], tBb, s_, sb_, s_, sb_, ALU.mult)
                tt(tC[:], tCb, c_, cb_, s_, sb_, ALU.mult)
                tt(c_, cb_, tA[:], tAb, tB[:], tBb, ALU.subtract)
                kb.op("dve", lambda e: e.tensor_scalar(out=s_, in0=tC[:], scalar1=2.0, scalar2=None, op0=ALU.mult), reads=[tCb], writes=[sb_])
            for _ in range(4):
                cdouble(cc_[:], ccb, ss[:], ssb)
            lbr, lbrb = T128(f"lbr{d}"); lbi, lbib = T128(f"lbi{d}")
            tt(lbr[:], lbrb, rr[:], rrb, cc_[:], ccb, ALU.mult)
            tt(lbi[:], lbib, rr[:], rrb, ss[:], ssb, ALU.mult)
            cre, creb = T128(f"cre{d}"); cim, cimb = T128(f"cim{d}"); den, denb = T128(f"den{d}")
            tt(tA[:], tAb, are[:], areb, are[:], areb, ALU.mult)
            tt(tB[:], tBb, aim[:], aimb, aim[:], aimb, ALU.mult)
            tt(den[:], denb, tA[:], tAb, tB[:], tBb, ALU.add)
            kb.op("dve", lambda e: e.reciprocal(out=den[:], in_=den[:]), reads=[denb], writes=[denb])
            kb.op("dve", lambda e: e.tensor_scalar(out=tC[:], in0=lbr[:], scalar1=-1.0, scalar2=None, op0=ALU.add), reads=[lbrb], writes=[tCb])
            tt(tA[:], tAb, tC[:], tCb, are[:], areb, ALU.mult)
            tt(tB[:], tBb, lbi[:], lbib, aim[:], aimb, ALU.mult)
            tt(tA[:], tAb, tA[:], tAb, tB[:], tBb, ALU.add)
            tt(cre[:], creb, tA[:], tAb, den[:], denb, ALU.mult)
            tt(tA[:], tAb, lbi[:], lbib, are[:], areb, ALU.mult)
            tt(tB[:], tBb, tC[:], tCb, aim[:], aimb, ALU.mult)
            tt(tA[:], tAb, tA[:], tAb, tB[:], tBb, ALU.subtract)
            tt(cim[:], cimb, tA[:], tAb, den[:], denb, ALU.mult)
            plr, plrb = T128(f"plr{d}"); pli, plib = T128(f"pli{d}")
            kb.op("act", lambda e: e.copy(out=plr[:], in_=lbr[:]), reads=[lbrb], writes=[plrb])
            kb.op("act", lambda e: e.copy(out=pli[:], in_=lbi[:]), reads=[lbib], writes=[plib])
            for _ in range(nsq):
                cdouble(plr[:], plrb, pli[:], plib)
            Hr, Hrb = T128(f"Hr{d}"); Hi, Hib = T128(f"Hi{d}"); Kr, Krb = T128(f"Kr{d}"); Ki, Kib = T128(f"Ki{d}")
            fl, flb = kb.sb([128, ncores], F32, name=f"fl{d}")
            if emit_y:
                kb.dma("sp", fl[:], flags_d[:, d, :], writes=[flb])
            kb.op("pool", lambda e: e.memset(Hr[:], 0.0), writes=[Hrb]); kb.op("pool", lambda e: e.memset(Hi[:], 0.0), writes=[Hib])
            kb.op("pool", lambda e: e.memset(Kr[:], 1.0), writes=[Krb]); kb.op("pool", lambda e: e.memset(Ki[:], 0.0), writes=[Kib])
            Ar, Arb = T128(f"Ar{d}"); Ai, Aib = T128(f"Ai{d}")
            order = (list(range(ncores)) if d == 0 else list(range(ncores - 1, -1, -1))) if emit_y else []
            for c2 in order:
                load_T(xall_d[c2, d, 0].rearrange("(t p) -> t p", p=128), Ar[:], Arb)
                load_T(xall_d[c2, d, 1].rearrange("(t p) -> t p", p=128), Ai[:], Aib)
                for (Xr, Xrb, Xi, Xib, addA) in ((Hr, Hrb, Hi, Hib, True), (Kr, Krb, Ki, Kib, False)):
                    tt(tA[:], tAb, plr[:], plrb, Xr[:], Xrb, ALU.mult)
                    tt(tB[:], tBb, pli[:], plib, Xi[:], Xib, ALU.mult)
                    tt(tA[:], tAb, tA[:], tAb, tB[:], tBb, ALU.subtract)
                    tt(tB[:], tBb, plr[:], plrb, Xi[:], Xib, ALU.mult)
                    tt(tC[:], tCb, pli[:], plib, Xr[:], Xrb, ALU.mult)
                    tt(tB[:], tBb, tB[:], tBb, tC[:], tCb, ALU.add)
                    if addA:
                        tt(tA[:], tAb, tA[:], tAb, Ar[:], Arb, ALU.add)
                        tt(tB[:], tBb, tB[:], tBb, Ai[:], Aib, ALU.add)
                    tt(tA[:], tAb, tA[:], tAb, Xr[:], Xrb, ALU.subtract)
                    tt(tB[:], tBb, tB[:], tBb, Xi[:], Xib, ALU.subtract)
                    kb.op("dve", lambda e: e.scalar_tensor_tensor(out=Xr[:], in0=tA[:], scalar=fl[:, c2:c2 + 1], in1=Xr[:], op0=ALU.mult, op1=ALU.add), reads=[tAb, flb, Xrb], writes=[Xrb])
                    kb.op("dve", lambda e: e.scalar_tensor_tensor(out=Xi[:], in0=tB[:], scalar=fl[:, c2:c2 + 1], in1=Xi[:], op0=ALU.mult, op1=ALU.add), reads=[tBb, flb, Xib], writes=[Xib])
            BB = []
            for comp in range(2):
                bbt, bbb = kb.sb([128, 128, 32], BF16, name=f"BB{d}{comp}")
                kb.op("pool", lambda e: e.memset(bbt[:], 0.0), writes=[bbb])
                X1, X1b, X2, X2b = (Bre, Breb, Bim, Bimb) if comp == 0 else (Bim, Bimb, Bre, Breb)
                kb.op("dve", lambda e: e.tensor_tensor(out=w1[:], in0=X1[:], in1=cre[:].unsqueeze(2).to_broadcast([128, 128, 16]), op=ALU.mult), reads=[X1b, creb], writes=[w1b])
                kb.op("dve", lambda e: e.tensor_tensor(out=w2[:], in0=X2[:], in1=cim[:].unsqueeze(2).to_broadcast([128, 128, 16]), op=ALU.mult), reads=[X2b, cimb], writes=[w2b])
                kb.op("dve", lambda e: e.tensor_tensor(out=w1[:], in0=w1[:], in1=w2[:], op=(ALU.subtract if comp == 0 else ALU.add)), reads=[w1b, w2b], writes=[w1b])
                kb.op("act", lambda e: e.copy(out=bbt[0:64, :, 0:16], in_=w1[0:64, :, :]), reads=[w1b, bbb], writes=[bbb])
                kb.op("act", lambda e: e.copy(out=bbt[64:128, :, 16:32], in_=w1[64:128, :, :]), reads=[w1b, bbb], writes=[bbb])
                BB.append((bbt, bbb))
            PR.append(dict(r=(rr, rrb), c=(cc_, ccb), s=(ss, ssb), H=(Hr, Hrb, Hi, Hib), K=(Kr, Krb, Ki, Kib), BB=BB))
        f0, f0b = kb.sb([128, 1], F32)
        if emit_y:
            kb.dma("sp", f0[:], f0_d[:, :], writes=[f0b])
        else:
            kb.op("pool", lambda e: e.memset(f0[:], 0.0), writes=[f0b])
        XF, XFb = kb.sb([128, 2, 2, 128], F32)
        CD = [kb.sb([32, 128], BF16, name=f"CD{i}") for i in range(4)]
        for (cdt, cdb) in CD:
            kb.op("pool", lambda e: e.memset(cdt[:], 0.0), writes=[cdb])
        CT = Rot([(kb.sb([128, 32], BF16), kb.sb([128, 32], BF16)) for _ in range(2)])
        BT = Rot([(kb.sb([32, 128], BF16), kb.sb([32, 128], BF16)) for _ in range(2)])
        Er, Erb = kb.sb([128, NE], F32); Ei, Eib = kb.sb([128, NE], F32)
        Rt, Rtb = kb.sb([128, NE], F32)
        ab, abb = kb.sb([128, 8], F32)
        U = Rot([kb.sb([32, T], BF16) for _ in range(2)])
        WIr, WIrb = kb.sb([128, NE], F32); WIi, WIib = kb.sb([128, NE], F32)
        Wr, Wrb = kb.sb([128, NE], F32); Wi, Wib = kb.sb([128, NE], F32)
        G1, G1b = kb.sb([128, NE], F32); G2, G2b = kb.sb([128, NE], F32)
        Xr_, Xrb_ = kb.sb([128, NE], BF16); Xi_, Xib_ = kb.sb([128, NE], BF16)
        YS = Rot([kb.sb([32, 512], F32) for _ in range(3)])
        st, stb = kb.sb([128, 16], F32)

        def scan_range(pr, t, d, u, ub, c0, n, init_zero, emit, y_d, y_b):
            (rr, rrb), (BTr, BTrb), (BTi, BTib) = pr
            for b0 in range(0, n, 512):
                m = min(512, n - b0)
                p1, p1b = P.next(); p2, p2b = P.next()
                kb.op("pe", lambda e: e.matmul(p1[:, 0:m], lhsT=BTr[:], rhs=u[:, c0 + b0:c0 + b0 + m], start=True, stop=True), reads=[BTrb, ub], writes=[p1b])
                kb.op("pe", lambda e: e.matmul(p2[:, 0:m], lhsT=BTi[:], rhs=u[:, c0 + b0:c0 + b0 + m], start=True, stop=True), reads=[BTib, ub], writes=[p2b])
                sl = slice(b0, b0 + m)
                kb.op("dve", lambda e: e.tensor_tensor(out=G1[:, sl], in0=p1[:, 0:m], in1=Er[:, sl], op=ALU.mult), reads=[p1b, Erb], writes=[G1b])
                kb.op("dve", lambda e: e.tensor_tensor(out=G2[:, sl], in0=p2[:, 0:m], in1=Ei[:, sl], op=ALU.mult), reads=[p2b, Eib], writes=[G2b])
                kb.op("pool", lambda e: e.tensor_tensor(out=WIr[:, sl], in0=G1[:, sl], in1=G2[:, sl], op=ALU.add), reads=[G1b, G2b], writes=[WIrb])
                kb.op("dve", lambda e: e.tensor_tensor(out=G1[:, sl], in0=p2[:, 0:m], in1=Er[:, sl], op=ALU.mult), reads=[p2b, Erb, WIrb], writes=[G1b])
                kb.op("dve", lambda e: e.tensor_tensor(out=G2[:, sl], in0=p1[:, 0:m], in1=Ei[:, sl], op=ALU.mult), reads=[p1b, Eib, WIrb], writes=[G2b])
                kb.op("pool", lambda e: e.tensor_tensor(out=WIi[:, sl], in0=G1[:, sl], in1=G2[:, sl], op=ALU.subtract), reads=[G1b, G2b], writes=[WIib])
            if init_zero:
                kb.op("pool", lambda e: e.memset(st[:, 2:4], 0.0), writes=[stb])
            else:
                kb.op("dve", lambda e: e.tensor_tensor(out=st[:, 4:5], in0=ab[:, 4:5], in1=st[:, 0:1], op=ALU.mult), reads=[abb, stb], writes=[stb])
                kb.op("dve", lambda e: e.tensor_tensor(out=st[:, 5:6], in0=ab[:, 5:6], in1=st[:, 1:2], op=ALU.mult), reads=[abb, stb], writes=[stb])
                kb.op("dve", lambda e: e.tensor_tensor(out=st[:, 2:3], in0=st[:, 4:5], in1=st[:, 5:6], op=ALU.subtract), reads=[stb], writes=[stb])
                kb.op("dve", lambda e: e.tensor_tensor(out=st[:, 4:5], in0=ab[:, 4:5], in1=st[:, 1:2], op=ALU.mult), reads=[abb, stb], writes=[stb])
                kb.op("dve", lambda e: e.tensor_tensor(out=st[:, 5:6], in0=ab[:, 5:6], in1=st[:, 0:1], op=ALU.mult), reads=[abb, stb], writes=[stb])
                kb.op("dve", lambda e: e.tensor_tensor(out=st[:, 3:4], in0=st[:, 4:5], in1=st[:, 5:6], op=ALU.add), reads=[stb], writes=[stb])
            kb.op("dve", lambda e: e.tensor_tensor_scan(out=Wr[:, 0:n], data0=Rt[:, 0:n], data1=WIr[:, 0:n], initial=st[:, 2:3], op0=ALU.mult, op1=ALU.add), reads=[Rtb, WIrb, stb], writes=[Wrb])
            kb.op("dve", lambda e: e.tensor_tensor_scan(out=Wi[:, 0:n], data0=Rt[:, 0:n], data1=WIi[:, 0:n], initial=st[:, 3:4], op0=ALU.mult, op1=ALU.add), reads=[Rtb, WIib, stb], writes=[Wib])
            e1 = slice(n - 1, n)
            kb.op("dve", lambda e: e.tensor_tensor(out=st[:, 4:5], in0=Wr[:, e1], in1=Er[:, e1], op=ALU.mult), reads=[Wrb, Erb, stb], writes=[stb])
            kb.op("dve", lambda e: e.tensor_tensor(out=st[:, 5:6], in0=Wi[:, e1], in1=Ei[:, e1], op=ALU.mult), reads=[Wib, Eib, stb], writes=[stb])
            kb.op("dve", lambda e: e.tensor_tensor(out=st[:, 8:9], in0=st[:, 4:5], in1=st[:, 5:6], op=ALU.subtract), reads=[stb], writes=[stb])
            kb.op("dve", lambda e: e.tensor_tensor(out=st[:, 4:5], in0=Wr[:, e1], in1=Ei[:, e1], op=ALU.mult), reads=[Wrb, Eib, stb], writes=[stb])
            kb.op("dve", lambda e: e.tensor_tensor(out=st[:, 5:6], in0=Wi[:, e1], in1=Er[:, e1], op=ALU.mult), reads=[Wib, Erb, stb], writes=[stb])
            kb.op("dve", lambda e: e.tensor_tensor(out=st[:, 9:10], in0=st[:, 4:5], in1=st[:, 5:6], op=ALU.add), reads=[stb], writes=[stb])
            if emit:
                (CTr, CTrb), (CTi, CTib) = emit
                kb.op("dve", lambda e: e.tensor_tensor(out=G1[:, 0:n], in0=Wr[:, 0:n], in1=Er[:, 0:n], op=ALU.mult), reads=[Wrb, Erb], writes=[G1b])
                kb.op("pool", lambda e: e.tensor_tensor(out=G2[:, 0:n], in0=Wi[:, 0:n], in1=Ei[:, 0:n], op=ALU.mult), reads=[Wib, Eib], writes=[G2b])
                kb.op("dve", lambda e: e.tensor_tensor(out=Xr_[:, 0:n], in0=G1[:, 0:n], in1=G2[:, 0:n], op=ALU.subtract), reads=[G1b, G2b], writes=[Xrb_])
                kb.op("dve", lambda e: e.tensor_tensor(out=G1[:, 0:n], in0=Wr[:, 0:n], in1=Ei[:, 0:n], op=ALU.mult), reads=[Wrb, Eib, Xrb_], writes=[G1b])
                kb.op("pool", lambda e: e.tensor_tensor(out=G2[:, 0:n], in0=Wi[:, 0:n], in1=Er[:, 0:n], op=ALU.mult), reads=[Wib, Erb, Xrb_], writes=[G2b])
                kb.op("dve", lambda e: e.tensor_tensor(out=Xi_[:, 0:n], in0=G1[:, 0:n], in1=G2[:, 0:n], op=ALU.add), reads=[G1b, G2b], writes=[Xib_])
                for b0 in range(0, n, 512):
                    m = min(512, n - b0)
                    py, pyb = P.next()
                    kb.op("pe", lambda e: e.matmul(py[0:32, 0:m], lhsT=CTr[:], rhs=Xr_[:, b0:b0 + m], start=True, stop=False), reads=[CTrb, Xrb_], writes=[pyb])
                    kb.op("pe", lambda e: e.matmul(py[0:32, 0:m], lhsT=CTi[:], rhs=Xi_[:, b0:b0 + m], start=False, stop=True), reads=[CTib, Xib_], writes=[pyb])
                    ys, ysb = YS.next()
                    kb.op("act", lambda e: e.copy(out=ys[:, 0:m], in_=py[0:32, 0:m]), reads=[pyb], writes=[ysb])
                    kb.dma("sp", y_d[t * 32:(t + 1) * 32, b0:b0 + m], ys[:, 0:m], reads=[ysb], writes=[y_b.sub((t, b0))])

        for t in range(128):
            (cdr, cdrb), (cdi, cdib) = CD[(t % 2) * 2], CD[(t % 2) * 2 + 1]
            ((CTr, CTrb), (CTi, CTib)) = CT.next()
            if emit_y:
                for gl in range(2):
                    kb.dma("pool", cdr[gl * 16:(gl + 1) * 16, gl * 64:(gl + 1) * 64], prm["c_re"][2 * t + gl], writes=[cdrb.sub(gl)])
                    kb.dma("pool", cdi[gl * 16:(gl + 1) * 16, gl * 64:(gl + 1) * 64], prm["c_im"][2 * t + gl], writes=[cdib.sub(gl)])
                for (cd_, cdb_, ct_, ctb_, sc_) in ((cdr, cdrb, CTr, CTrb, 1.0), (cdi, cdib, CTi, CTib, -1.0)):
                    ptb, ptbb = PB.next()
                    kb.op("pe", lambda e: e.transpose(out=ptb[:, 0:32], in_=cd_[:], identity=identb[0:32, 0:32]), reads=[cdb_, identbb], writes=[ptbb])
                    kb.op("act", lambda e: e.activation(out=ct_[:], in_=ptb[:, 0:32], func=AF.Copy, scale=sc_), reads=[ptbb], writes=[ctb_])
            for d in range(2):
                pr = PR[d]
                (rr, rrb), (cc_, ccb), (ss, ssb) = pr["r"], pr["c"], pr["s"]
                ((BTr, BTrb), (BTi, BTib)) = BT.next()
                for comp, (bt_, btb_) in enumerate(((BTr, BTrb), (BTi, BTib))):
                    bbt, bbb = pr["BB"][comp]
                    ptb, ptbb = PB.next()
                    kb.op("pe", lambda e: e.transpose(out=ptb[0:32, 0:128], in_=bbt[:, t, :], identity=identb[:]), reads=[bbb, identbb], writes=[ptbb])
                    kb.op("act", lambda e: e.copy(out=bt_[:], in_=ptb[0:32, 0:128]), reads=[ptbb], writes=[btb_])
                kb.op("act", lambda e: e.copy(out=ab[:, 0:1], in_=cc_[:, t:t + 1]), reads=[ccb, abb], writes=[abb])
                kb.op("act", lambda e: e.copy(out=ab[:, 1:2], in_=ss[:, t:t + 1]), reads=[ssb, abb], writes=[abb])
                kb.op("act", lambda e: e.copy(out=ab[:, 4:5], in_=cc_[:, t:t + 1]), reads=[ccb, abb], writes=[abb])
                kb.op("act", lambda e: e.copy(out=ab[:, 5:6], in_=ss[:, t:t + 1]), reads=[ssb, abb], writes=[abb])
                kb.op("pool", lambda e: e.memset(Er[:, 0:1], 1.0), writes=[Erb])
                kb.op("pool", lambda e: e.memset(Ei[:, 0:1], 0.0), writes=[Eib])
                n_ = 1
                while n_ < NE:
                    m = min(n_, NE - n_)
                    kb.op("dve", lambda e: e.tensor_scalar(out=G1[:, 0:m], in0=Ei[:, 0:m], scalar1=ab[:, 1:2], scalar2=None, op0=ALU.mult), reads=[Eib, abb], writes=[G1b])
                    kb.op("dve", lambda e: e.scalar_tensor_tensor(out=G2[:, 0:m], in0=Er[:, 0:m], scalar=ab[:, 0:1], in1=G1[:, 0:m], op0=ALU.mult, op1=ALU.subtract), reads=[Erb, abb, G1b], writes=[G2b])
                    kb.op("dve", lambda e: e.tensor_scalar(out=G1[:, 0:m], in0=Ei[:, 0:m], scalar1=ab[:, 0:1], scalar2=None, op0=ALU.mult), reads=[Eib, abb, G2b], writes=[G1b])
                    kb.op("dve", lambda e: e.scalar_tensor_tensor(out=Ei[:, n_:n_ + m], in0=Er[:, 0:m], scalar=ab[:, 1:2], in1=G1[:, 0:m], op0=ALU.mult, op1=ALU.add), reads=[Erb, abb, G1b], writes=[Eib])
                    kb.op("act", lambda e: e.copy(out=Er[:, n_:n_ + m], in_=G2[:, 0:m]), reads=[G2b], writes=[Erb])
                    kb.op("dve", lambda e: e.tensor_tensor(out=ab[:, 2:3], in0=ab[:, 0:1], in1=ab[:, 0:1], op=ALU.mult), reads=[abb], writes=[abb])
                    kb.op("dve", lambda e: e.tensor_tensor(out=ab[:, 3:4], in0=ab[:, 1:2], in1=ab[:, 1:2], op=ALU.mult), reads=[abb], writes=[abb])
                    kb.op("dve", lambda e: e.tensor_tensor(out=ab[:, 6:7], in0=ab[:, 0:1], in1=ab[:, 1:2], op=ALU.mult), reads=[abb], writes=[abb])
                    kb.op("dve", lambda e: e.tensor_tensor(out=ab[:, 0:1], in0=ab[:, 2:3], in1=ab[:, 3:4], op=ALU.subtract), reads=[abb], writes=[abb])
                    kb.op("dve", lambda e: e.tensor_scalar(out=ab[:, 1:2], in0=ab[:, 6:7], scalar1=2.0, scalar2=None, op0=ALU.mult), reads=[abb], writes=[abb])
                    n_ *= 2
                kb.op("dve", lambda e: e.tensor_scalar(out=Rt[:], in0=ones[:], scalar1=rr[:, t:t + 1], scalar2=None, op0=ALU.mult), reads=[onesb, rrb], writes=[Rtb])
                u, ub = U.next()
                src = uT_d if d == 0 else uR_d
                srcb = uT_b if d == 0 else uR_b
                kb.dma("pool", u[:], src[t * 32:(t + 1) * 32, :], reads=[srcb], writes=[ub])
                prt = ((rr, rrb), (BTr, BTrb), (BTi, BTib))
                cctx0 = LT if d == 0 else 0
                clat0 = 0 if d == 0 else CTX
                scan_range(prt, t, d, u, ub, cctx0, CTX, True, None, None, None)
                Hr, Hrb, Hi, Hib = pr["H"]; Kr, Krb, Ki, Kib = pr["K"]
                kb.op("dve", lambda e: e.tensor_tensor(out=st[:, 4:5], in0=Kr[:, t:t + 1], in1=st[:, 8:9], op=ALU.mult), reads=[Krb, stb], writes=[stb])
                kb.op("dve", lambda e: e.tensor_tensor(out=st[:, 5:6], in0=Ki[:, t:t + 1], in1=st[:, 9:10], op=ALU.mult), reads=[Kib, stb], writes=[stb])
                kb.op("dve", lambda e: e.tensor_tensor(out=st[:, 6:7], in0=st[:, 4:5], in1=st[:, 5:6], op=ALU.subtract), reads=[stb], writes=[stb])
                kb.op("dve", lambda e: e.tensor_tensor(out=st[:, 4:5], in0=Kr[:, t:t + 1], in1=st[:, 9:10], op=ALU.mult), reads=[Krb, stb], writes=[stb])
                kb.op("dve", lambda e: e.tensor_tensor(out=st[:, 5:6], in0=Ki[:, t:t + 1], in1=st[:, 8:9], op=ALU.mult), reads=[Kib, stb], writes=[stb])
                kb.op("dve", lambda e: e.tensor_tensor(out=st[:, 7:8], in0=st[:, 4:5], in1=st[:, 5:6], op=ALU.add), reads=[stb], writes=[stb])
                kb.op("dve", lambda e: e.scalar_tensor_tensor(out=st[:, 0:1], in0=st[:, 6:7], scalar=f0[:, 0:1], in1=Hr[:, t:t + 1], op0=ALU.mult, op1=ALU.add), reads=[stb, f0b, Hrb], writes=[stb])
                kb.op("dve", lambda e: e.scalar_tensor_tensor(out=st[:, 1:2], in0=st[:, 7:8], scalar=f0[:, 0:1], in1=Hi[:, t:t + 1], op0=ALU.mult, op1=ALU.add), reads=[stb, f0b, Hib], writes=[stb])
                scan_range(prt, t, d, u, ub, clat0, LT, False, (((CTr, CTrb), (CTi, CTib)) if emit_y else None), yf_d if d == 0 else yr_d, yf_b if d == 0 else yr_b)
                kb.op("act", lambda e: e.copy(out=XF[:, d, 0, t:t + 1], in_=st[:, 8:9]), reads=[stb], writes=[XFb.sub((d, 0, t))])
                kb.op("act", lambda e: e.copy(out=XF[:, d, 1, t:t + 1], in_=st[:, 9:10]), reads=[stb], writes=[XFb.sub((d, 1, t))])
        XO = Rot([kb.sb([128, 128], F32) for _ in range(2)])
        for d in range(2):
            for comp in range(2):
                pt, pb = P.next()
                kb.op("pe", lambda e: e.transpose(out=pt[:, 0:128], in_=XF[:, d, comp, :], identity=ident[:]), reads=[XFb, identb32], writes=[pb])
                xo, xob = XO.next()
                kb.op("act", lambda e: e.copy(out=xo[:], in_=pt[:, 0:128]), reads=[pb], writes=[xob])
                kb.dma("sp", xout_d[d, comp].rearrange("(t p) -> t p", p=128), xo[:], reads=[xob], writes=[xout_b.sub((d, comp))])


def s5_gelu_phase(kb, LT, yf_d, yf_b, yb_d, yb_b, uT_d, uT_b, dskip_d, yg_d, yg_b):
    with phase(kb):
        dcol, dcolb = kb.sb([128, KT], F32)
        kb.dma("sp", dcol[:], dskip_d.rearrange("(kt p) -> p kt", p=128), writes=[dcolb], allow_slow_non_contiguous=True)
        A_ = Rot([kb.sb([128, 512], F32) for _ in range(2)]); B_ = Rot([kb.sb([128, 512], F32) for _ in range(2)])
        C_ = Rot([kb.sb([128, 512], F32) for _ in range(2)]); W_ = Rot([kb.sb([128, 512], F32) for _ in range(2)])
        O_ = Rot([kb.sb([128, 512], BF16) for _ in range(2)])
        for ct in range(KT):
            rows = slice(ct * 128, (ct + 1) * 128)
            for b0 in range(0, LT, 512):
                m = min(512, LT - b0)
                a, ab_ = A_.next(); b, bb_ = B_.next(); c, cb_ = C_.next(); w, wb_ = W_.next(); o, ob_ = O_.next()
                kb.dma("sp", a[:, 0:m], yf_d[rows, b0:b0 + m], reads=[yf_b], writes=[ab_])
                kb.dma("sp", b[:, 0:m], yb_d[rows, b0:b0 + m], reads=[yb_b], writes=[bb_])
                kb.dma("sp", c[:, 0:m], uT_d[rows, b0:b0 + m], reads=[uT_b], writes=[cb_])
                kb.op("dve", lambda e: e.tensor_tensor(out=a[:, 0:m], in0=a[:, 0:m], in1=b[:, 0:m], op=ALU.add), reads=[ab_, bb_], writes=[ab_])
                kb.op("dve", lambda e: e.scalar_tensor_tensor(out=a[:, 0:m], in0=c[:, 0:m], scalar=dcol[:, ct:ct + 1], in1=a[:, 0:m], op0=ALU.mult, op1=ALU.add), reads=[cb_, dcolb, ab_], writes=[ab_])
                kb.op("pool", lambda e: e.tensor_tensor(out=w[:, 0:m], in0=a[:, 0:m], in1=a[:, 0:m], op=ALU.mult), reads=[ab_], writes=[wb_])
                kb.op("dve", lambda e: e.tensor_scalar(out=w[:, 0:m], in0=w[:, 0:m], scalar1=0.044715, scalar2=1.0, op0=ALU.mult, op1=ALU.add), reads=[wb_], writes=[wb_])
                kb.op("pool", lambda e: e.tensor_tensor(out=w[:, 0:m], in0=w[:, 0:m], in1=a[:, 0:m], op=ALU.mult), reads=[wb_, ab_], writes=[wb_])
                kb.op("act", lambda e: e.activation(out=w[:, 0:m], in_=w[:, 0:m], func=AF.Sigmoid, scale=1.5957691216057308), reads=[wb_], writes=[wb_])
                kb.op("dve", lambda e: e.tensor_tensor(out=o[:, 0:m], in0=w[:, 0:m], in1=a[:, 0:m], op=ALU.mult), reads=[wb_, ab_], writes=[ob_])
                kb.dma("sp", yg_d[rows, b0:b0 + m], o[:, 0:m], reads=[ob_], writes=[yg_b.sub((ct, b0))])


def s5_glu(kb, LT, yg_d, yg_b, zT_d, zT_b, wglu_d, bglu_d, ogT_d, ogT_b, P):
    with phase(kb):
        bg, bgb = kb.sb([128, KT], F32)
        kb.dma("sp", bg[:], bglu_d.rearrange("(kt p) -> p kt", p=128), writes=[bgb], allow_slow_non_contiguous=True)
        G_ = Rot([kb.sb([128, 512], F32) for _ in range(3)])
        Y_ = Rot([kb.sb([128, 512], BF16) for _ in range(3)])
        Z_ = Rot([kb.sb([128, 512], BF16) for _ in range(3)])
        O_ = Rot([kb.sb([128, 512], BF16) for _ in range(3)])

        def evac(ps, pb, c0, t0, n):
            g, gb = G_.next(); y, yb = Y_.next(); z, zb = Z_.next(); o, ob = O_.next()
            ct = c0 // 128
            kb.dma("sp", y[:, 0:n], yg_d[c0:c0 + 128, t0:t0 + n], reads=[yg_b], writes=[yb])
            kb.dma("sp", z[:, 0:n], zT_d[c0:c0 + 128, t0:t0 + n], reads=[zT_b], writes=[zb])
            kb.op("act", lambda e: e.activation(out=g[:, 0:n], in_=ps, func=AF.Sigmoid, bias=bg[:, ct:ct + 1], scale=1.0), reads=[pb, bgb], writes=[gb])
            kb.op("dve", lambda e: e.tensor_tensor(out=g[:, 0:n], in0=g[:, 0:n], in1=y[:, 0:n], op=ALU.mult), reads=[gb, yb], writes=[gb])
            kb.op("pool", lambda e: e.tensor_tensor(out=o[:, 0:n], in0=g[:, 0:n], in1=z[:, 0:n], op=ALU.mult), reads=[gb, zb], writes=[ob])
            kb.dma("sp", ogT_d[c0:c0 + 128, t0:t0 + n], o[:, 0:n], reads=[ob], writes=[ogT_b.sub((c0, t0))])
        gemm(kb, None, yg_d, yg_b, D, (0, LT), wglu_d, list(range(0, D, 128)), "fm", evac, psum_pool=P)


def build_layer3(ncores, LT, states_only=False):
    T = LT + CTX
    full = not states_only
    nc = bass.Bass("TRN2", target_bir_lowering=False)

    def inp(name, shape, dt=F32):
        return nc.dram_tensor(name, list(shape), dt, kind="ExternalInput").ap()
    xs = inp("xs", [LT, D]); ctx = inp("ctx", [CTX, D]); c = inp("c", [D]); cc = inp("cc", [D])
    modw = inp("modw", [D, 3 * D]); modb = inp("modb", [3 * D])
    lng = inp("lng", [D]) if full else None; lnb = inp("lnb", [D]) if full else None
    w_in = inp("w_in", [D, 8192 if full else 4096])
    w_glu = inp("w_glu", [D, D]) if full else None; b_glu = inp("b_glu", [D]) if full else None
    w_out = inp("w_out", [D, D]) if full else None; dskip = inp("dskip", [D]) if full else None
    prm = dict(a_re=inp("a_re", [2, 256, 64]), a_im=inp("a_im", [2, 256, 64]), ldt_exp=inp("ldt_exp", [2, 16384]),
               b_re=inp("b_re", [256, 64, 16]), b_im=inp("b_im", [256, 64, 16]))
    if full:
        prm["c_re"] = inp("c_re", [256, 16, 64]); prm["c_im"] = inp("c_im", [256, 16, 64])
    xall = inp("xall", [ncores, 2, 2, 16384]) if full else None
    f0 = inp("f0", [128, 1]) if full else None
    flags = inp("flags", [128, 2, ncores]) if full else None
    ident32 = inp("ident32", [128, 128]); identb = inp("identb", [128, 128], BF16); J32 = inp("J32", [128, 128])
    xo = nc.dram_tensor("xo", [LT, D], F32, kind="ExternalOutput").ap() if full else None
    xout = nc.dram_tensor("xout", [2, 2, 16384], F32, kind="ExternalOutput").ap()
    with contextlib.ExitStack() as es:
        kb = KB(nc, es)
        P = Rot([kb.ps([128, 512], F32) for _ in range(6)])
        PB = Rot([kb.ps([128, 1024], BF16) for _ in range(2)])
        ident, identb32 = kb.sb([128, 128], F32)
        kb.dma("sp", ident[:], ident32[:, :], writes=[identb32])
        J, Jb = kb.sb([128, 128], F32)
        kb.dma("sp", J[:], J32[:, :], writes=[Jb])
        ext = kb.buf("ext")
        modrow_d, modrow_b = kb.dram([2, 3 * D], F32)
        hT_d, hT_b = kb.dram([D, T], BF16)
        uT_d, uT_b = kb.dram([D, T], F32)
        uR_d, uR_b = kb.dram([D, T], F32)
        zT_d, zT_b = kb.dram([D, LT], BF16)
        yf_d, yf_b = kb.dram([D, LT], F32)
        yr_d, yr_b = kb.dram([D, LT], F32)
        yb_d, yb_b = kb.dram([D, LT], F32)
        yg_d, yg_b = kb.dram([D, LT], BF16)
        ogT_d, ogT_b = kb.dram([D, LT], BF16)
        y_d, y_b = kb.dram([LT, D], F32)
        xo_b = kb.buf("xo"); xout_b = kb.buf("xout")
        mod_phase(kb, c, cc, modw, modb, modrow_d, modrow_b, P)
        prep(kb, xs, ext, LT // 128, modrow_d, modrow_b, 0, hT_d, hT_b, 0, ident, identb32, P)
        prep(kb, ctx, ext, CTX // 128, modrow_d, modrow_b, 1, hT_d, hT_b, LT, ident, identb32, P)
        proj_fm(kb, hT_d, hT_b, (0, T), w_in, 0, 4096, uT_d, uT_b, F32, P)
        if full:
            proj_fm(kb, hT_d, hT_b, (0, LT), w_in, 4096, 4096, zT_d, zT_b, BF16, P, func=AF.Silu)
        reverse_fm(kb, uT_d, uT_b, uR_d, uR_b, D, T, ident, identb32, J, Jb, P)
        s5_mixer(kb, ncores, LT, uT_d, uT_b, uR_d, uR_b, yf_d, yf_b, yr_d, yr_b, prm, xall, f0, flags, xout, xout_b, ident, identb32, identb, P, PB, emit_y=full)
        if full:
            reverse_fm(kb, yr_d, yr_b, yb_d, yb_b, D, LT, ident, identb32, J, Jb, P)
            s5_gelu_phase(kb, LT, yf_d, yf_b, yb_d, yb_b, uT_d, uT_b, dskip, yg_d, yg_b)
            s5_glu(kb, LT, yg_d, yg_b, zT_d, zT_b, w_glu, b_glu, ogT_d, ogT_b, P)
            outproj(kb, ogT_d, ogT_b, D, LT, w_out, y_d, y_b, P)
            epilogue(kb, xs, ext, [(y_d, y_b)], LT // 128, 0, modrow_d, modrow_b, 0, lng, lnb, xo, xo_b)
        kb.finish()
    return nc


def run_layer3(ncores, LT, x_in, ctx_in, c, c_ctx, mod_w, mod_b, ln_g, ln_b, p):
    base = []
    ldt_exp = np.ascontiguousarray(np.repeat(np.asarray(p["s5_log_dt"], np.float32), 64, axis=-1))
    for ci in range(ncores):
        m = dict(xs=np.ascontiguousarray(x_in[ci * LT:(ci + 1) * LT]), ctx=ctx_in, c=c, cc=c_ctx, modw=mod_w, modb=mod_b,
                 a_re=p["s5_a_re"], a_im=p["s5_a_im"], ldt_exp=ldt_exp, b_re=p["s5_b_re"], b_im=p["s5_b_im"])
        cs = s5_consts(ncores, LT, ci)
        m.update(ident32=cs["ident32"], identb=cs["identb"], J32=cs["J32"])
        base.append((m, cs))
    w_u = np.ascontiguousarray(p["s5_w_in"][:, :4096])
    nc1 = build_layer3(ncores, LT, states_only=True)
    res = run_bass_kernel_spmd(nc1, [dict(m, w_in=w_u) for (m, cs) in base], core_ids=list(range(ncores)))
    xall = np.stack([r["xout"] for r in res.results], 0)
    del res
    nc = build_layer3(ncores, LT)
    f1 = np.ones((128, 1), np.float32)
    res = run_bass_kernel_spmd(nc, [dict(m, w_in=p["s5_w_in"], w_glu=p["s5_w_glu"], b_glu=p["s5_b_glu"], w_out=p["s5_w_out"], dskip=p["s5_d"],
                                         c_re=p["s5_c_re"], c_im=p["s5_c_im"], xall=xall, f0=f1, flags=cs["flags"], lng=ln_g, lnb=ln_b) for (m, cs) in base],
                               core_ids=list(range(ncores)))
    return np.concatenate([r["xo"] for r in res.results], 0)


NCORES = 8


def kernel(x, c, ctx, c_ctx, mod_w, mod_b, ln_g, ln_b, wa_w_in, wa_w_out, wa_sink, rt_w_in, rt_w_out, rt_log_decay,
           na_w_in, na_w_out, na_rpb, s5_w_in, s5_a_re, s5_a_im, s5_log_dt, s5_b_re, s5_b_im, s5_c_re, s5_c_im,
           s5_d, s5_w_glu, s5_b_glu, s5_w_out):
    f = lambda a: np.ascontiguousarray(np.asarray(a, np.float32))
    x = f(x); L = x.shape[1]; LT = L // NCORES
    cv = f(c)[0]; ccv = f(c_ctx); mod_w = f(mod_w); mod_b = f(mod_b); ln_g = f(ln_g); ln_b = f(ln_b)
    nc = build_layer0(LT)
    slabs = halo_slabs(x[0], NCORES, LT, 128)
    in_maps = []
    for ci in range(NCORES):
        m = dict(xs=slabs[ci], ctx=f(ctx)[0], c=cv, cc=ccv, modw=mod_w[0], modb=mod_b[0], lng=ln_g[0], lnb=ln_b[0],
                 w_in=f(wa_w_in)[0], w_out=f(wa_w_out)[0], sink=f(wa_sink)[0])
        m.update(wa_consts(NCORES, LT, ci))
        in_maps.append(m)
    res = run_bass_kernel_spmd(nc, in_maps, core_ids=list(range(NCORES)))
    xcur = np.concatenate([r["xo"] for r in res.results], 0)
    ccur = res.results[0]["co"]
    del res, in_maps, slabs
    xcur, ccur = run_layer1(NCORES, LT, xcur, ccur, cv, ccv, mod_w[1], mod_b[1], ln_g[1], ln_b[1], f(rt_w_in)[0], f(rt_w_out)[0], f(rt_log_decay)[0])
    xcur, ccur = run_layer2(NCORES, LT, xcur, ccur, cv, ccv, mod_w[2], mod_b[2], ln_g[2], ln_b[2], f(na_w_in)[0], f(na_w_out)[0], f(na_rpb)[0])
    p = dict(s5_w_in=f(s5_w_in)[0], s5_a_re=f(s5_a_re)[0], s5_a_im=f(s5_a_im)[0], s5_log_dt=f(s5_log_dt)[0], s5_b_re=f(s5_b_re)[0], s5_b_im=f(s5_b_im)[0],
             s5_c_re=f(s5_c_re)[0], s5_c_im=f(s5_c_im)[0], s5_d=f(s5_d)[0], s5_w_glu=f(s5_w_glu)[0], s5_b_glu=f(s5_b_glu)[0], s5_w_out=f(s5_w_out)[0])
    xcur = run_layer3(NCORES, LT, xcur, ccur, cv, ccv, mod_w[3], mod_b[3], ln_g[3], ln_b[3], p)
    return xcur.reshape(1, L, D).astype(np.float32)
```
